# Optimizing a Trainium2 kernel written in Bass

```python
import math
import jax
import jax.numpy as jnp
from jax import lax
import numpy as np


D_MODEL = 1024
BATCH = 8
SEQ = 4096
DEPTH = 2

CHUNK = 64
Q_BLOCK = 128
EPS = 1e-6
N_BRANCHES = 4
BRANCH_WIDTH = D_MODEL // 4
HEADS = 4
HEAD_DIM = BRANCH_WIDTH // HEADS
D_FF = 4 * D_MODEL

DN_DK = HEAD_DIM
DN_DV = HEAD_DIM
CONV_K = 4
MLA_NOPE = HEAD_DIM
MLA_ROPE = HEAD_DIM // 2
MLA_V = HEAD_DIM
MLA_Q_LORA = D_MODEL // 4
MLA_KV_LORA = D_MODEL // 8
ROPE_THETA = 10000.0
GLA_DK = HEAD_DIM // 2
GLA_DV = HEAD_DIM
GLA_RANK = 16
GLA_TAU = 16.0
FOX_DH = HEAD_DIM

IN_SIZES = (
    HEADS * (2 * DN_DK + DN_DV), HEADS * DN_DV, HEADS, HEADS,
    MLA_Q_LORA, MLA_KV_LORA, MLA_ROPE,
    HEADS * GLA_DK, HEADS * GLA_DK, HEADS * GLA_DV, HEADS * GLA_DV, GLA_RANK,
    HEADS * FOX_DH, HEADS * FOX_DH, HEADS * FOX_DH, HEADS,
)
IN_WIDTH = sum(IN_SIZES)

kernel_name = 'hybrid_gated_parallel_mixer_block'


def rms_norm(x, gain):
    xf = x.astype(jnp.float32)
    y = xf * lax.rsqrt(jnp.mean(xf * xf, axis=-1, keepdims=True) + EPS)
    return (y * gain.astype(jnp.float32)).astype(x.dtype)


def l2_normalize(x):
    xf = x.astype(jnp.float32)
    return (xf * lax.rsqrt(jnp.sum(xf * xf, axis=-1, keepdims=True) + EPS)).astype(x.dtype)


def causal_depthwise_conv(x, w):
    k = w.shape[0]
    return lax.conv_general_dilated(
        x, w[:, None, :].astype(x.dtype), window_strides=(1,), padding=((k - 1, 0),),
        dimension_numbers=('NWC', 'WIO', 'NWC'), feature_group_count=x.shape[-1])


def to_chunks(t):
    b, s, h, d = t.shape
    return t.reshape(b, s // CHUNK, CHUNK, h, d).transpose(0, 3, 1, 2, 4)


def from_chunks(t):
    b, h, n, c, d = t.shape
    return t.transpose(0, 2, 3, 1, 4).reshape(b, n * c, h, d)


def scalar_chunks(t):
    b, s, h = t.shape
    return t.reshape(b, s // CHUNK, CHUNK, h).transpose(0, 3, 1, 2)


def gated_delta_rule(q, k, v, g, beta):
    out_dtype = v.dtype
    f32 = jnp.float32
    b_sz, s_len, n_h, dk = q.shape
    dv = v.shape[-1]
    qc = to_chunks(q.astype(f32)) * (dk ** -0.5)
    kc = to_chunks(k.astype(f32))
    vc = to_chunks(v.astype(f32))
    gc = jnp.cumsum(scalar_chunks(g.astype(f32)), axis=-1)
    bc = scalar_chunks(beta.astype(f32))
    lower = jnp.tril(jnp.ones((CHUNK, CHUNK), dtype=bool))
    strict = jnp.tril(jnp.ones((CHUNK, CHUNK), dtype=bool), -1)
    diff = gc[..., :, None] - gc[..., None, :]
    decay_ts = jnp.where(lower, jnp.exp(jnp.where(lower, diff, 0.0)), 0.0)
    kk = jnp.einsum('bhncd,bhnsd->bhncs', kc, kc)
    a_strict = jnp.where(strict, bc[..., None] * kk * decay_ts, 0.0)
    eye = jnp.eye(CHUNK, dtype=f32)
    rhs = jnp.concatenate([vc * bc[..., None], kc * (bc * jnp.exp(gc))[..., None]], axis=-1)
    sol = lax.linalg.triangular_solve(a_strict + eye, rhs, left_side=True, lower=True,
                                      unit_diagonal=True)
    u = sol[..., :dv]
    w = sol[..., dv:]
    qk = jnp.einsum('bhncd,bhnsd->bhncs', qc, kc) * decay_ts
    q_decay = qc * jnp.exp(gc)[..., None]
    k_tail = kc * jnp.exp(gc[..., -1:] - gc)[..., None]
    chunk_decay = jnp.exp(gc[..., -1])

    def step(state, inp):
        u_c, w_c, qk_c, qd_c, kt_c, cd_c = inp
        v_new = u_c - jnp.einsum('bhck,bhkv->bhcv', w_c, state)
        o_c = (jnp.einsum('bhck,bhkv->bhcv', qd_c, state)
               + jnp.einsum('bhcs,bhsv->bhcv', qk_c, v_new))
        state = state * cd_c[..., None, None] + jnp.einsum('bhck,bhcv->bhkv', kt_c, v_new)
        return state, o_c

    xs = tuple(jnp.moveaxis(t, 2, 0) for t in (u, w, qk, q_decay, k_tail, chunk_decay))
    s0 = jnp.zeros((b_sz, n_h, dk, dv), f32)
    _, o = lax.scan(step, s0, xs)
    return from_chunks(jnp.moveaxis(o, 0, 2)).astype(out_dtype)


def gla_chunked(q, k, v, log_a):
    out_dtype = v.dtype
    f32 = jnp.float32
    b_sz, s_len, n_h, dk = q.shape
    dv = v.shape[-1]
    qc = to_chunks(q.astype(f32)) * (dk ** -0.5)
    kc = to_chunks(k.astype(f32))
    vc = to_chunks(v.astype(f32))
    bcum = jnp.cumsum(to_chunks(log_a.astype(f32)), axis=3)
    q_in = qc * jnp.exp(bcum)
    k_in = kc * jnp.exp(-bcum)
    lower = jnp.tril(jnp.ones((CHUNK, CHUNK), dtype=bool))
    attn = jnp.where(lower, jnp.einsum('bhncd,bhnsd->bhncs', q_in, k_in), 0.0)
    o_intra = jnp.einsum('bhncs,bhnsv->bhncv', attn, vc)
    b_last = bcum[..., -1:, :]
    k_tail = kc * jnp.exp(b_last - bcum)
    chunk_decay = jnp.exp(b_last[..., 0, :])

    def step(state, inp):
        qi_c, kt_c, v_c, cd_c = inp
        o_c = jnp.einsum('bhck,bhkv->bhcv', qi_c, state)
        state = state * cd_c[..., None] + jnp.einsum('bhck,bhcv->bhkv', kt_c, v_c)
        return state, o_c

    xs = tuple(jnp.moveaxis(t, 2, 0) for t in (q_in, k_tail, vc, chunk_decay))
    s0 = jnp.zeros((b_sz, n_h, dk, dv), f32)
    _, o_inter = lax.scan(step, s0, xs)
    o = o_intra + jnp.moveaxis(o_inter, 0, 2)
    return from_chunks(o).astype(out_dtype)


def rope_cos_sin(seq, dim):
    inv_freq = 1.0 / (ROPE_THETA ** (jnp.arange(0, dim, 2, dtype=jnp.float32) / dim))
    ang = jnp.arange(seq, dtype=jnp.float32)[:, None] * inv_freq[None, :]
    ang = jnp.concatenate([ang, ang], axis=-1)
    return jnp.cos(ang), jnp.sin(ang)


def apply_rope(x, cos, sin):
    xf = x.astype(jnp.float32)
    x1, x2 = jnp.split(xf, 2, axis=-1)
    rot = jnp.concatenate([-x2, x1], axis=-1)
    return (xf * cos[None, :, None, :] + rot * sin[None, :, None, :]).astype(x.dtype)


def block_sweep_attention(q, k, v, scale, forget_cum):
    s_len = q.shape[1]
    outs = []
    for start in range(0, s_len, Q_BLOCK):
        end = start + Q_BLOCK
        logits = jnp.einsum('bqhd,bkhd->bhqk', q[:, start:end], k[:, :end],
                            preferred_element_type=jnp.float32) * scale
        q_pos = jnp.arange(start, end)
        k_pos = jnp.arange(end)
        if forget_cum is None:
            allowed = (k_pos[None, :] // CHUNK) <= (q_pos[:, None] // CHUNK)
        else:
            allowed = k_pos[None, :] <= q_pos[:, None]
            f_q = forget_cum[:, start:end].transpose(0, 2, 1)
            f_k = forget_cum[:, :end].transpose(0, 2, 1)
            logits = logits + (f_q[..., :, None] - f_k[..., None, :])
        logits = jnp.where(allowed, logits, -jnp.inf)
        p = jax.nn.softmax(logits, axis=-1)
        outs.append(jnp.einsum('bhqk,bkhd->bqhd', p.astype(v.dtype), v[:, :end]))
    return jnp.concatenate(outs, axis=1)


def hybrid_layer(x, norm_mix_pre, norm_mix_post, norm_mlp_pre, norm_mlp_post, w_in,
                 dn_conv, dn_a_log, dn_dt_bias, dn_norm,
                 mla_q_norm, mla_w_q_up, mla_kv_norm, mla_w_kv_up,
                 gla_w_gate_up, gla_gate_bias, gla_norm, fox_f_bias,
                 w_branch_up, w_gate, b_gate, w_out, w_mlp_in, w_mlp_out):
    f32 = jnp.float32
    b_sz, s_len, _ = x.shape
    h = rms_norm(x, norm_mix_pre)
    proj = h @ w_in
    split_at = np.cumsum(IN_SIZES)[:-1].tolist()
    (dn_qkv, dn_z, dn_b, dn_a, mla_cq, mla_ckv, mla_kpe,
     gla_q, gla_k, gla_v, gla_g, gla_lr,
     fox_q, fox_k, fox_v, fox_f) = jnp.split(proj, split_at, axis=-1)

    qkv = jax.nn.silu(causal_depthwise_conv(dn_qkv, dn_conv))
    a_q, a_k, a_v = jnp.split(qkv, [HEADS * DN_DK, 2 * HEADS * DN_DK], axis=-1)
    a_q = l2_normalize(a_q.reshape(b_sz, s_len, HEADS, DN_DK))
    a_k = l2_normalize(a_k.reshape(b_sz, s_len, HEADS, DN_DK))
    a_v = a_v.reshape(b_sz, s_len, HEADS, DN_DV)
    beta = jax.nn.sigmoid(dn_b.astype(f32))
    g = -jnp.exp(dn_a_log.astype(f32)) * jax.nn.softplus(dn_a.astype(f32) + dn_dt_bias.astype(f32))
    o_a = gated_delta_rule(a_q, a_k, a_v, g, beta)
    o_a = (rms_norm(o_a, dn_norm) * jax.nn.silu(dn_z.reshape(b_sz, s_len, HEADS, DN_DV))
           ).reshape(b_sz, s_len, BRANCH_WIDTH)

    cq = rms_norm(mla_cq, mla_q_norm)
    b_q = (cq @ mla_w_q_up).reshape(b_sz, s_len, HEADS, MLA_NOPE + MLA_ROPE)
    q_nope, q_pe = jnp.split(b_q, [MLA_NOPE], axis=-1)
    ckv = rms_norm(mla_ckv, mla_kv_norm)
    b_kv = (ckv @ mla_w_kv_up).reshape(b_sz, s_len, HEADS, MLA_NOPE + MLA_V)
    k_nope, b_v = jnp.split(b_kv, [MLA_NOPE], axis=-1)
    cos, sin = rope_cos_sin(s_len, MLA_ROPE)
    q_pe = apply_rope(q_pe, cos, sin)
    k_pe = apply_rope(mla_kpe[:, :, None, :], cos, sin)
    b_qf = jnp.concatenate([q_nope, q_pe], axis=-1)
    b_kf = jnp.concatenate([k_nope, jnp.broadcast_to(k_pe, (b_sz, s_len, HEADS, MLA_ROPE))], axis=-1)
    o_b = block_sweep_attention(b_qf, b_kf, b_v, (MLA_NOPE + MLA_ROPE) ** -0.5, None)
    o_b = o_b.reshape(b_sz, s_len, BRANCH_WIDTH)

    log_a = jax.nn.log_sigmoid((gla_lr @ gla_w_gate_up + gla_gate_bias).astype(f32)) / GLA_TAU
    o_c = gla_chunked(gla_q.reshape(b_sz, s_len, HEADS, GLA_DK),
                      gla_k.reshape(b_sz, s_len, HEADS, GLA_DK),
                      gla_v.reshape(b_sz, s_len, HEADS, GLA_DV),
                      log_a.reshape(b_sz, s_len, HEADS, GLA_DK))
    o_c = (rms_norm(o_c, gla_norm) * jax.nn.silu(gla_g.reshape(b_sz, s_len, HEADS, GLA_DV))
           ).reshape(b_sz, s_len, BRANCH_WIDTH)

    log_f = jax.nn.log_sigmoid((fox_f + fox_f_bias).astype(f32))
    forget_cum = jnp.cumsum(log_f, axis=1)
    o_d = block_sweep_attention(fox_q.reshape(b_sz, s_len, HEADS, FOX_DH),
                                fox_k.reshape(b_sz, s_len, HEADS, FOX_DH),
                                fox_v.reshape(b_sz, s_len, HEADS, FOX_DH),
                                FOX_DH ** -0.5, forget_cum)
    o_d = o_d.reshape(b_sz, s_len, BRANCH_WIDTH)

    branches = (o_a, o_b, o_c, o_d)
    merged = jnp.zeros_like(x)
    for i in range(N_BRANCHES):
        gate = jax.nn.sigmoid(h @ w_gate[i] + b_gate[i])
        merged = merged + gate * (branches[i] @ w_branch_up[i])
    x = x + rms_norm(merged @ w_out, norm_mix_post)

    h2 = rms_norm(x, norm_mlp_pre)
    u = jax.nn.relu(h2 @ w_mlp_in)
    x = x + rms_norm((u * u) @ w_mlp_out, norm_mlp_post)
    return x


def setup_inputs(seed: int = 0) -> dict:
    key = jax.random.key(seed)
    ks = jax.random.split(key, 24)
    L, D, f32 = DEPTH, D_MODEL, jnp.float32

    def nrm(i, shape, scale):
        return jax.random.normal(ks[i], shape, f32) * scale

    def gain(i, shape):
        return 1.0 + 0.1 * jax.random.normal(ks[i], shape, f32)

    conv_width = HEADS * (2 * DN_DK + DN_DV)
    dt = jnp.exp(jax.random.uniform(ks[7], (L, HEADS), f32, math.log(1e-3), math.log(1e-1)))
    return {
        'x': nrm(0, (BATCH, SEQ, D), 1.0),
        'norm_mix_pre': gain(1, (L, D)),
        'norm_mix_post': gain(2, (L, D)),
        'norm_mlp_pre': gain(3, (L, D)),
        'norm_mlp_post': gain(4, (L, D)),
        'w_in': nrm(5, (L, D, IN_WIDTH), D ** -0.5),
        'dn_conv': nrm(8, (L, CONV_K, conv_width), CONV_K ** -0.5),
        'dn_a_log': jnp.log(jax.random.uniform(ks[6], (L, HEADS), f32, 1.0, 16.0)),
        'dn_dt_bias': dt + jnp.log(-jnp.expm1(-dt)),
        'dn_norm': gain(9, (L, DN_DV)),
        'mla_q_norm': gain(10, (L, MLA_Q_LORA)),
        'mla_w_q_up': nrm(11, (L, MLA_Q_LORA, HEADS * (MLA_NOPE + MLA_ROPE)), MLA_Q_LORA ** -0.5),
        'mla_kv_norm': gain(12, (L, MLA_KV_LORA)),
        'mla_w_kv_up': nrm(13, (L, MLA_KV_LORA, HEADS * (MLA_NOPE + MLA_V)), MLA_KV_LORA ** -0.5),
        'gla_w_gate_up': nrm(14, (L, GLA_RANK, HEADS * GLA_DK), GLA_RANK ** -0.5),
        'gla_gate_bias': nrm(15, (L, HEADS * GLA_DK), 0.1),
        'gla_norm': gain(16, (L, GLA_DV)),
        'fox_f_bias': jax.random.uniform(ks[17], (L, HEADS), f32, 1.0, 3.0),
        'w_branch_up': nrm(18, (L, N_BRANCHES, BRANCH_WIDTH, D), BRANCH_WIDTH ** -0.5),
        'w_gate': nrm(19, (L, N_BRANCHES, D, D), D ** -0.5),
        'b_gate': nrm(20, (L, N_BRANCHES, D), 0.1),
        'w_out': nrm(21, (L, D, D), D ** -0.5),
        'w_mlp_in': nrm(22, (L, D, D_FF), D ** -0.5),
        'w_mlp_out': nrm(23, (L, D_FF, D), D_FF ** -0.5),
    }


def reference(x, norm_mix_pre, norm_mix_post, norm_mlp_pre, norm_mlp_post, w_in,
              dn_conv, dn_a_log, dn_dt_bias, dn_norm,
              mla_q_norm, mla_w_q_up, mla_kv_norm, mla_w_kv_up,
              gla_w_gate_up, gla_gate_bias, gla_norm, fox_f_bias,
              w_branch_up, w_gate, b_gate, w_out, w_mlp_in, w_mlp_out):
    for l in range(DEPTH):
        x = hybrid_layer(x, norm_mix_pre[l], norm_mix_post[l], norm_mlp_pre[l], norm_mlp_post[l],
                         w_in[l], dn_conv[l], dn_a_log[l], dn_dt_bias[l], dn_norm[l],
                         mla_q_norm[l], mla_w_q_up[l], mla_kv_norm[l], mla_w_kv_up[l],
                         gla_w_gate_up[l], gla_gate_bias[l], gla_norm[l], fox_f_bias[l],
                         w_branch_up[l], w_gate[l], b_gate[l], w_out[l], w_mlp_in[l], w_mlp_out[l])
    return x
```

```python
import contextlib
import os
import numpy as np
import concourse.bass as bass
import concourse.mybir as mybir
from concourse.bass_utils import run_bass_kernel_spmd

F32 = mybir.dt.float32
BF16 = mybir.dt.bfloat16
ALU = mybir.AluOpType
AF = mybir.ActivationFunctionType
AX = mybir.AxisListType

T = 4096
D = 1024
NT = 32
DEPTH = 2
EPS = 1e-6
IN_W = 3004
O_QKV, O_Z, O_B, O_A, O_CQ, O_CKV, O_KPE = 0, 768, 1024, 1028, 1032, 1288, 1416
O_GQ, O_GK, O_GV, O_GG, O_GLR = 1448, 1576, 1704, 1960, 2216
O_FQ, O_FK, O_FV, O_FF = 2232, 2488, 2744, 3000
BIG = 30000.0

(C_ID, C_CAUS, C_ONES, C_BLKI, C_SAME, C_NCAUS, C_NCHUNK, C_DTS, C_DSTS, C_DSTI,
 C_SH0, C_SH1, C_SH2, C_SH3, C_SP1, C_SP2, C_SP3) = range(17)
NCM = 17


def kstop(n):
    return int(os.environ.get('KSTOP', '99')) <= n


class Buf:
    __slots__ = ("name", "w", "r", "excl")

    def __init__(self, name="", excl=False):
        self.name = name
        self.w = None
        self.r = {}
        self.excl = excl


class FW:
    N_LANES = 8

    def __init__(self, nc, stack):
        self.nc = nc
        self.stack = stack
        self.sems = {}
        self.cnt = {}
        self.waited = {}
        self.engs = {"pe": nc.tensor, "act": nc.scalar, "dve": nc.vector,
                     "pool": nc.gpsimd, "sp": nc.sync}
        for e in self.engs:
            self._mksem(e)
            self.waited[e] = {}
        self.lane_next = {}
        for q in ("sp", "pool", "act"):
            self.lane_next[q] = 0
            for i in range(self.N_LANES):
                self._mksem(f"dma_{q}_{i}")
        self.n_inst = 0
        self.n_wait = 0

    def _mksem(self, key):
        self.sems[key] = self.stack.enter_context(self.nc.semaphore(key))
        self.cnt[key] = 0

    def _wait(self, eng, tok):
        if tok is None:
            return
        key, val = tok
        if key == eng and eng == "pe":
            return
        w = self.waited[eng]
        if w.get(key, 0) >= val:
            return
        self.engs[eng].wait_ge(self.sems[key], val)
        w[key] = val
        self.n_wait += 1

    def _deps(self, eng, reads, writes):
        for b in reads:
            self._wait(eng, b.w)
            if b.excl:
                for k, v in b.r.items():
                    if k != eng:
                        self._wait(eng, (k, v))
        for b in writes:
            self._wait(eng, b.w)
            for k, v in b.r.items():
                self._wait(eng, (k, v))

    def _mark(self, tok, reads, writes):
        k, v = tok
        for b in reads:
            if b.r.get(k, 0) < v:
                b.r[k] = v
        for b in writes:
            b.w = tok
            b.r = {}

    def op(self, eng, inst_fn, reads=(), writes=()):
        self._deps(eng, reads, writes)
        ins = inst_fn(self.engs[eng])
        self.cnt[eng] += 1
        ins.then_inc(self.sems[eng], 1)
        tok = (eng, self.cnt[eng])
        self._mark(tok, reads, writes)
        self.n_inst += 1
        return tok

    def dma(self, q, out, in_, reads=(), writes=(), **kw):
        lane = self.lane_next[q]
        self.lane_next[q] = (lane + 1) % self.N_LANES
        key = f"dma_{q}_{lane}"
        self._deps(q, reads, writes)
        self._wait(q, (key, self.cnt[key]))
        ins = self.engs[q].dma_start(out=out, in_=in_, **kw)
        self.cnt[key] += 16
        ins.then_inc(self.sems[key], 16)
        tok = (key, self.cnt[key])
        self._mark(tok, reads, writes)
        self.n_inst += 1
        return tok

    def barrier(self):
        for e in self.engs:
            for key, v in self.cnt.items():
                if v > 0:
                    self._wait(e, (key, v))

    def final_wait(self):
        for key, v in self.cnt.items():
            if v > 0:
                self._wait("sp", (key, v))


def run_interleaved(gens):
    active = list(gens)
    while active:
        nxt = []
        for g in active:
            try:
                next(g)
                nxt.append(g)
            except StopIteration:
                pass
        active = nxt


def run_pipelined(gen_fns, depth):
    pending = list(gen_fns)
    free = list(range(depth))
    active = []
    while pending or active:
        while pending and free:
            sl = free.pop(0)
            active.append((pending.pop(0)(sl), sl))
        nxt = []
        for g, sl in active:
            try:
                next(g)
                nxt.append((g, sl))
            except StopIteration:
                free.append(sl)
        active = nxt


class TB:
    def __init__(self, t, name, excl=False):
        self.t = t
        self.b = Buf(name, excl)

    def __getitem__(self, k):
        return self.t[k]


class Ring:
    def __init__(self, items):
        self.items = items
        self.i = 0

    def next(self):
        it = self.items[self.i % len(self.items)]
        self.i += 1
        return it


def host_consts():
    s = np.arange(128)[:, None]
    t = np.arange(128)[None, :]
    same = (s // 64) == (t // 64)
    cm = np.zeros((NCM, 128, 128), np.float32)
    cm[C_ID] = (s == t)
    cm[C_CAUS] = (s <= t)
    cm[C_ONES] = 1.0
    cm[C_BLKI] = same & (s <= t)
    cm[C_SAME] = same
    cm[C_NCAUS] = np.where(s <= t, 0.0, -BIG)
    cm[C_NCHUNK] = np.where((s // 64) <= (t // 64), 0.0, -BIG)
    cm[C_DTS] = np.where(same & (t < s), 0.0, -BIG)
    cm[C_DSTS] = np.where(same & (s < t), 0.0, BIG)
    cm[C_DSTI] = np.where(same & (s <= t), 0.0, BIG)
    for d in range(4):
        cm[C_SH0 + d] = (t == s + d)
    for d in range(1, 4):
        cm[C_SP1 + d - 1] = (t == s + d - 128)
    p = np.arange(128)[:, None]
    cv = np.zeros((128, 4 + 256), np.float32)
    cv[:, 0:4] = (p // 32) == np.arange(4)[None, :]
    cv[:, 4:] = (p // 32) == (np.arange(256)[None, :] // 64)
    inv_freq = 1.0 / (10000.0 ** (np.arange(0, 32, 2, dtype=np.float32) / 32))
    ang = np.arange(T, dtype=np.float32)[:, None] * inv_freq[None, :]
    ang = np.concatenate([ang, ang], axis=-1).astype(np.float32)
    cos = np.cos(ang).astype(np.float32)
    sin = np.sin(ang).astype(np.float32)
    sgn = np.concatenate([-np.ones(16), np.ones(16)]).astype(np.float32)
    rope = np.concatenate([np.tile(cos, (1, 4)), np.tile(sin * sgn[None], (1, 4))], axis=1)
    return cm, cv, np.ascontiguousarray(rope.astype(np.float32))


def build(n_layers=DEPTH, debug=(), passes="1ABCD34"):
    nc = bass.Bass("TRN2", target_bir_lowering=False)

    def din(name, shape, dt=F32):
        return nc.dram_tensor(name, list(shape), dt, kind="ExternalInput").ap()

    x_in = din("x", [T, D])
    L = DEPTH
    W = {
        "norm_mix_pre": din("norm_mix_pre", [L, D]),
        "norm_mix_post": din("norm_mix_post", [L, D]),
        "norm_mlp_pre": din("norm_mlp_pre", [L, D]),
        "norm_mlp_post": din("norm_mlp_post", [L, D]),
        "w_in": din("w_in", [L, D, IN_W]),
        "dn_conv": din("dn_conv", [L, 4, 768]),
        "dn_a_log": din("dn_a_log", [L, 4]),
        "dn_dt_bias": din("dn_dt_bias", [L, 4]),
        "dn_norm": din("dn_norm", [L, 64]),
        "mla_q_norm": din("mla_q_norm", [L, 256]),
        "mla_w_q_up": din("mla_w_q_up", [L, 256, 384]),
        "mla_kv_norm": din("mla_kv_norm", [L, 128]),
        "mla_w_kv_up": din("mla_w_kv_up", [L, 128, 512]),
        "gla_w_gate_up": din("gla_w_gate_up", [L, 16, 128]),
        "gla_gate_bias": din("gla_gate_bias", [L, 128]),
        "gla_norm": din("gla_norm", [L, 64]),
        "fox_f_bias": din("fox_f_bias", [L, 4]),
        "w_branch_up": din("w_branch_up", [L, 4, 256, D]),
        "w_gate": din("w_gate", [L, 4, D, D]),
        "b_gate": din("b_gate", [L, 4, D]),
        "w_out": din("w_out", [L, D, D]),
        "w_mlp_in": din("w_mlp_in", [L, D, 4 * D]),
        "w_mlp_out": din("w_mlp_out", [L, 4 * D, D]),
    }
    cmat_d = din("cmat", [NCM, 128, 128])
    cvec_d = din("cvec", [128, 260])
    rope_d = din("rope", [T, 256])
    out_d = nc.dram_tensor("out", [T, D], F32, kind="ExternalOutput").ap()

    def dscr(name, shape, dt=F32):
        kind = "ExternalOutput" if name in debug else "Internal"
        return nc.dram_tensor(name, list(shape), dt, kind=kind).ap()

    x1_d = dscr("x1s", [T, D])
    x2_d = dscr("x2s", [T, D])
    hT_d = dscr("hTs", [8, 128, T], BF16)
    proj_d = dscr("projs", [T, IN_W])
    obr_d = dscr("obrs", [T, D], BF16)

    with contextlib.ExitStack() as gst:
        fw = FW(nc, gst)

        uid = [0]

        def mk_alloc(st):
            def sb(name, shape, dt=F32):
                uid[0] += 1
                name = f"{name}_{uid[0]}"
                return TB(st.enter_context(nc.sbuf_tensor(name, list(shape), dt)), name)

            def ps(name, shape=(128, 512), dt=F32):
                uid[0] += 1
                name = f"{name}_{uid[0]}"
                return TB(st.enter_context(nc.psum_tensor(name, list(shape), dt)), name, True)
            return sb, ps

        gsb, _ = mk_alloc(gst)
        cm = gsb("cm", [128, NCM, 128])
        cmb = gsb("cmb", [128, NCM, 128], BF16)
        cv = gsb("cv", [128, 260])
        fw.dma("sp", cm[:], cmat_d.rearrange("n p f -> p n f"), writes=[cm.b])
        fw.dma("sp", cv[:], cvec_d, writes=[cv.b])
        fw.op("dve", lambda e: e.tensor_copy(out=cmb[:], in_=cm[:]), [cm.b], [cmb.b])
        ident = cm[:, C_ID, :]
        identb = cmb[:, C_ID, :]

        def rstd_from_ssq(ssq_ap, out_ap, n, bufs, eps=EPS, mul=None):
            m = (1.0 / n) if mul is None else mul
            fw.op("dve", lambda e: e.tensor_scalar(out=out_ap, in0=ssq_ap, scalar1=m, scalar2=eps,
                                                   op0=ALU.mult, op1=ALU.add), bufs, bufs)
            fw.op("act", lambda e: e.activation(out=out_ap, in_=out_ap, func=AF.Sqrt), bufs, bufs)
            fw.op("dve", lambda e: e.reciprocal(out=out_ap, in_=out_ap), bufs, bufs)

        def load_cast(dst_tb, dst_ap, src_ap, stg_ring, eng, scale_ap=None, scale_bufs=()):
            stg = stg_ring.next()
            n = src_ap.shape[-1]
            fw.dma("sp", stg[:, 0:n], src_ap, writes=[stg.b])
            if scale_ap is None:
                if eng == "act":
                    fw.op("act", lambda e: e.copy(out=dst_ap, in_=stg[:, 0:n]), [stg.b], [dst_tb.b])
                else:
                    fw.op(eng, lambda e: e.tensor_copy(out=dst_ap, in_=stg[:, 0:n]), [stg.b], [dst_tb.b])
            else:
                if eng == "act":
                    fw.op("act", lambda e: e.activation(out=dst_ap, in_=stg[:, 0:n], func=AF.Copy, scale=scale_ap),
                          [stg.b] + list(scale_bufs), [dst_tb.b])
                else:
                    fw.op(eng, lambda e: e.tensor_scalar(out=dst_ap, in0=stg[:, 0:n], scalar1=scale_ap, scalar2=None,
                                                         op0=ALU.mult), [stg.b] + list(scale_bufs), [dst_tb.b])

        def col_vectors(st, l, pcv=None):
            sb, ps = mk_alloc(st)
            rows = sb("cvrows", [19, 128])
            colv = sb("colv", [128, 19])
            tst = None
            if pcv is None:
                tst = contextlib.ExitStack()
                _, ps_t = mk_alloc(tst)
                pcv = ps_t("pcv", [128, 512])
            fw.dma("sp", rows[0:8, :], W["norm_mix_pre"][l].rearrange("(c p) -> c p", p=128), writes=[rows.b])
            fw.dma("sp", rows[8:16, :], W["norm_mlp_pre"][l].rearrange("(c p) -> c p", p=128), writes=[rows.b])
            fw.dma("sp", rows[16:18, :], W["mla_q_norm"][l].rearrange("(c p) -> c p", p=128), writes=[rows.b])
            fw.dma("sp", rows[18:19, :], W["mla_kv_norm"][l].rearrange("(c p) -> c p", p=128), writes=[rows.b])
            fw.op("pe", lambda e: e.transpose(out=pcv[:, 0:19], in_=rows[:], identity=ident[0:19, 0:19]),
                  [rows.b, cm.b], [pcv.b])
            fw.op("dve", lambda e: e.tensor_copy(out=colv[:], in_=pcv[:, 0:19]), [pcv.b], [colv.b])
            if tst is not None:
                fw.barrier()
                tst.close()
            return colv

        def norm_tile_to_T(xt, xs, hTt, pT, stat, junk):
            fw.op("act", lambda e: e.activation(out=junk[:], in_=xt[:], func=AF.Square, accum_out=stat[:, 0:1]),
                  [xt.b], [junk.b, stat.b])
            rstd_from_ssq(stat[:, 0:1], stat[:, 1:2], D, [stat.b])
            fw.op("dve", lambda e: e.tensor_scalar(out=xs[:], in0=xt[:], scalar1=stat[:, 1:2], scalar2=None,
                                                   op0=ALU.mult), [xt.b, stat.b], [xs.b])
            for c in range(8):
                fw.op("pe", lambda e: e.transpose(out=pT[:, c * 128:(c + 1) * 128], in_=xs[:, c * 128:(c + 1) * 128],
                                                  identity=identb), [xs.b, cmb.b], [pT.b])
            fw.op("act", lambda e: e.copy(out=hTt[:].rearrange("p a b -> p (a b)"), in_=pT[:]), [pT.b], [hTt.b])

        def pass1(l, xsrc):
            with contextlib.ExitStack() as st:
                sb, ps = mk_alloc(st)
                pp = [ps(f"pp{i}") for i in range(6)]
                colv = col_vectors(st, l, pp[0])
                wbf = sb("w_in_bf", [128, 8, IN_W], BF16)
                stg = Ring([sb(f"stg{i}", [128, IN_W]) for i in range(2)])
                for k in range(8):
                    load_cast(wbf, wbf[:, k, :], W["w_in"][l, k * 128:(k + 1) * 128, :], stg,
                              "act" if k % 2 else "dve", colv[:, k:k + 1], [colv.b])

                def mkslot(i):
                    return dict(xt=sb(f"xt{i}", [128, D]), xs=sb(f"xs{i}", [128, D], BF16), hTt=sb(f"hTt{i}", [128, 8, 128], BF16),
                                pj=sb(f"pj{i}", [128, IN_W]), stat=sb(f"stat{i}", [128, 2]), pT=ps(f"pT1{i}", [128, 1024], BF16))
                slots = [mkslot(i) for i in range(2)]

                def tile_gen(t, sl):
                    d_ = slots[sl]
                    xt = d_["xt"]; xs = d_["xs"]; hTt = d_["hTt"]; pj = d_["pj"]; stat = d_["stat"]; pT = d_["pT"]
                    fw.dma("sp", xt[:], xsrc[t * 128:(t + 1) * 128, :], writes=[xt.b])
                    yield
                    for _ in norm_gen(xt, xs, hTt, pT, stat):
                        yield
                    fw.dma("pool", hT_d[:, :, t * 128:(t + 1) * 128].rearrange("c p n -> p c n"), hTt[:],
                           reads=[hTt.b])
                    for nb in range(6):
                        c0 = nb * 512
                        w = min(512, IN_W - c0)
                        pb = pp[3 * sl + nb % 3]
                        for k in range(8):
                            fw.op("pe", lambda e: e.matmul(pb[:, 0:w], lhsT=hTt[:, k, :], rhs=wbf[:, k, c0:c0 + w],
                                                           start=(k == 0), stop=(k == 7)),
                                  [hTt.b, wbf.b], [pb.b])
                        if nb % 2 == 0:
                            fw.op("act", lambda e: e.copy(out=pj[:, c0:c0 + w], in_=pb[:, 0:w]), [pb.b], [pj.b])
                        else:
                            fw.op("dve", lambda e: e.tensor_copy(out=pj[:, c0:c0 + w], in_=pb[:, 0:w]),
                                  [pb.b], [pj.b])
                        yield
                    fw.dma("pool", proj_d[t * 128:(t + 1) * 128, :], pj[:], reads=[pj.b])

                run_pipelined([(lambda sl, t=t: tile_gen(t, sl)) for t in range(NT)], 2)
                fw.barrier()

        def head_norm_gate_store(sb_o, o_tb, zsrc_ap, zsrc_bufs, gain_bc, tmp, t, col0):
            sq, st4, sg, ob = tmp
            fw.op("pool", lambda e: e.tensor_tensor(out=sq[:], in0=o_tb[:], in1=o_tb[:], op=ALU.mult), [o_tb.b], [sq.b])
            fw.op("dve", lambda e: e.tensor_reduce(out=st4[:, 0:4], in_=sq[:].rearrange("p (h d) -> p h d", h=4),
                                                   axis=AX.X, op=ALU.add), [sq.b], [st4.b])
            rstd_from_ssq(st4[:, 0:4], st4[:, 4:8], 64, [st4.b])
            fw.op("act", lambda e: e.activation(out=sg[:], in_=zsrc_ap, func=AF.Silu), zsrc_bufs, [sg.b])
            fw.op("pool", lambda e: e.tensor_tensor(out=sg[:], in0=sg[:], in1=gain_bc[:], op=ALU.mult),
                  [sg.b, gain_bc.b], [sg.b])
            for h in range(4):
                fw.op("dve", lambda e: e.scalar_tensor_tensor(out=ob[:, h * 64:(h + 1) * 64], in0=o_tb[:, h * 64:(h + 1) * 64],
                                                              scalar=st4[:, 4 + h:5 + h], in1=sg[:, h * 64:(h + 1) * 64],
                                                              op0=ALU.mult, op1=ALU.mult),
                      [o_tb.b, st4.b, sg.b], [ob.b])
            fw.dma("pool", obr_d[t * 128:(t + 1) * 128, col0:col0 + 256], ob[:], reads=[ob.b])

        def head_norm_gate_gen(o_tb, zsrc_ap, zsrc_bufs, gain_bc, tmp, t, col0):
            sq, st4, sg, ob = tmp
            fw.op("act", lambda e: e.activation(out=sg[:], in_=zsrc_ap, func=AF.Silu), zsrc_bufs, [sg.b])
            yield
            fw.op("pool", lambda e: e.tensor_tensor(out=sg[:], in0=sg[:], in1=gain_bc[:], op=ALU.mult),
                  [sg.b, gain_bc.b], [sg.b])
            fw.op("pool", lambda e: e.tensor_tensor(out=sq[:], in0=o_tb[:], in1=o_tb[:], op=ALU.mult), [o_tb.b], [sq.b])
            yield
            fw.op("dve", lambda e: e.tensor_reduce(out=st4[:, 0:4], in_=sq[:].rearrange("p (h d) -> p h d", h=4),
                                                   axis=AX.X, op=ALU.add), [sq.b], [st4.b])
            yield
            rstd_from_ssq(st4[:, 0:4], st4[:, 4:8], 64, [st4.b])
            yield
            for h in range(4):
                fw.op("dve", lambda e: e.scalar_tensor_tensor(out=ob[:, h * 64:(h + 1) * 64], in0=o_tb[:, h * 64:(h + 1) * 64],
                                                              scalar=st4[:, 4 + h:5 + h], in1=sg[:, h * 64:(h + 1) * 64],
                                                              op0=ALU.mult, op1=ALU.mult),
                      [o_tb.b, st4.b, sg.b], [ob.b])
            yield
            fw.dma("pool", obr_d[t * 128:(t + 1) * 128, col0:col0 + 256], ob[:], reads=[ob.b])

        def bcast_load(sb, name, src_ap, n, reps=1):
            tb = sb(name, [128, n * reps])
            for r in range(reps):
                fw.dma("sp", tb[:, r * n:(r + 1) * n], src_ap.partition_broadcast(128), writes=[tb.b])
            return tb

        def passA(l):
            with contextlib.ExitStack() as st:
                sb, ps = mk_alloc(st)
                convw = sb("convw", [128, 4, 768])
                for j in range(4):
                    fw.dma("sp", convw[:, j, :], W["dn_conv"][l, j].partition_broadcast(128), writes=[convw.b])
                gain_bc = bcast_load(sb, "dn_gain", W["dn_norm"][l], 64, 4)
                alog = bcast_load(sb, "alog", W["dn_a_log"][l], 4)
                dtb = bcast_load(sb, "dtb", W["dn_dt_bias"][l], 4)
                fw.op("act", lambda e: e.activation(out=alog[:], in_=alog[:], func=AF.Exp), [alog.b], [alog.b])
                xwr = Ring([sb(f"xw{i}", [128, 4, 768], BF16) for i in range(3)])

                def mkslotA(i):
                    return dict(pj=sb(f"pjA{i}", [128, 1032]), qkv=sb(f"qkvc{i}", [128, 768]), sq=sb(f"sqA{i}", [128, 512]),
                                sc=sb(f"scA{i}", [128, 64]), qn=sb(f"qn{i}", [128, 256]), kn=sb(f"kn{i}", [128, 256]),
                                kb=sb(f"kb{i}", [128, 256]), kbd=sb(f"kbd{i}", [128, 256]), vb=sb(f"vb{i}", [128, 256]),
                                qd=sb(f"qd{i}", [128, 256]), kt=sb(f"kt{i}", [128, 256]), cd=sb(f"cd{i}", [64, 2, 4]),
                                o=sb(f"oA{i}", [128, 256]))
                PA = [mkslotA(i) for i in range(2)]
                TTh = [sb(f"TT{h}", [64, 4, 128]) for h in range(4)]
                gdh = [sb(f"gd{h}", [128, 128]) for h in range(4)]
                dech = [[sb(f"dec{h}_{i}", [128, 128]) for i in range(3)] for h in range(4)]
                Mrh = [Ring([sb(f"M{h}_{i}", [128, 128]) for i in range(3)]) for h in range(4)]
                Nrh = [Ring([sb(f"N{h}_{i}", [128, 128]) for i in range(3)]) for h in range(4)]
                Qrh = [Ring([sb(f"Q{h}_{i}", [128, 128]) for i in range(3)]) for h in range(4)]
                qkmh = [sb(f"qkm{h}", [128, 128]) for h in range(4)]
                uh = [sb(f"u{h}", [128, 64]) for h in range(4)]
                wTh = [sb(f"wT{h}", [64, 128]) for h in range(4)]
                vnewh = [sb(f"vnew{h}", [128, 64]) for h in range(4)]
                S = [sb(f"S{h}", [64, 64]) for h in range(4)]
                tmp = (sb("sqo", [128, 256]), sb("st4", [128, 8]), sb("sg", [128, 256]), sb("ob", [128, 256], BF16))
                pc = [ps("pc0"), ps("pc1")]
                pM = ps("pM")
                ph = [ps(f"ph{h}") for h in range(4)]
                for h in range(4):
                    fw.op("pool", lambda e: e.memset(S[h][:], 0.0), [], [S[h].b])
                for h in range(4):
                    fw.op("pool", lambda e: e.memset(vnewh[h][:], 0.0), [], [vnewh[h].b])
                xw_hist = {}

                def pre_gen(t):
                    d_ = PA[t % 2]
                    pj = d_["pj"]; qkv = d_["qkv"]; sq = d_["sq"]; sc = d_["sc"]; qn = d_["qn"]; kn = d_["kn"]; kb = d_["kb"]
                    kbd = d_["kbd"]; vb = d_["vb"]; qd = d_["qd"]; kt = d_["kt"]; cd = d_["cd"]
                    xw_prev = xw_hist.get(t - 1)
                    fw.dma("sp", pj[:], proj_d[t * 128:(t + 1) * 128, 0:1032], writes=[pj.b])
                    xw = xwr.next()
                    xw_hist[t] = xw
                    yield
                    for d in range(4):
                        fw.op("pool" if d % 2 else "dve",
                              lambda e: e.tensor_tensor(out=xw[:, d, :], in0=pj[:, 0:768], in1=convw[:, 3 - d, :], op=ALU.mult),
                              [pj.b, convw.b], [xw.b])
                    yield
                    for hf in range(2):
                        cs = slice(hf * 384, (hf + 1) * 384)
                        nmm = 4 + (3 if xw_prev is not None else 0)
                        i = 0
                        for d in range(4):
                            fw.op("pe", lambda e: e.matmul(pc[hf][:, 0:384], lhsT=cmb[:, C_SH0 + d, :], rhs=xw[:, d, cs],
                                                           start=(i == 0), stop=(i == nmm - 1)), [cmb.b, xw.b], [pc[hf].b])
                            i += 1
                        if xw_prev is not None:
                            for d in range(1, 4):
                                fw.op("pe", lambda e: e.matmul(pc[hf][:, 0:384], lhsT=cmb[:, C_SP1 + d - 1, :],
                                                               rhs=xw_prev[:, d, cs], start=False, stop=(i == nmm - 1)),
                                      [cmb.b, xw_prev.b], [pc[hf].b])
                                i += 1
                        fw.op("act", lambda e: e.activation(out=qkv[:, cs], in_=pc[hf][:, 0:384], func=AF.Silu),
                              [pc[hf].b], [qkv.b])
                    yield
                    fw.op("pool", lambda e: e.tensor_tensor(out=sq[:], in0=qkv[:, 0:512], in1=qkv[:, 0:512], op=ALU.mult),
                          [qkv.b], [sq.b])
                    fw.op("dve", lambda e: e.tensor_reduce(out=sc[:, 0:8], in_=sq[:].rearrange("p (h d) -> p h d", h=8),
                                                           axis=AX.X, op=ALU.add), [sq.b], [sc.b])
                    yield
                    rstd_from_ssq(sc[:, 0:8], sc[:, 8:16], 1, [sc.b], eps=EPS, mul=1.0)
                    fw.op("act", lambda e: e.activation(out=sc[:, 16:20], in_=pj[:, O_B:O_B + 4], func=AF.Sigmoid),
                          [pj.b], [sc.b])
                    fw.op("dve", lambda e: e.tensor_tensor(out=sc[:, 40:44], in0=pj[:, O_A:O_A + 4], in1=dtb[:], op=ALU.add),
                          [pj.b, dtb.b], [sc.b])
                    fw.op("act", lambda e: e.activation(out=sc[:, 40:44], in_=sc[:, 40:44], func=AF.Exp), [sc.b], [sc.b])
                    fw.op("act", lambda e: e.activation(out=sc[:, 40:44], in_=sc[:, 40:44], func=AF.Ln, bias=1.0),
                          [sc.b], [sc.b])
                    fw.op("dve", lambda e: e.tensor_tensor(out=sc[:, 20:24], in0=sc[:, 40:44], in1=alog[:], op=ALU.mult),
                          [sc.b, alog.b], [sc.b])
                    yield
                    fw.op("pe", lambda e: e.matmul(pM[:, 0:4], lhsT=cm[:, C_BLKI, :], rhs=sc[:, 20:24], start=True, stop=True),
                          [cm.b, sc.b], [pM.b])
                    fw.op("pe", lambda e: e.matmul(pM[:, 4:8], lhsT=cm[:, C_SAME, :], rhs=sc[:, 20:24], start=True, stop=True),
                          [cm.b, sc.b], [pM.b])
                    for c in range(2):
                        fw.op("pe", lambda e: e.matmul(pM[0:64, 8 + 4 * c:12 + 4 * c], lhsT=cm[:, C_SAME, 64 * c:64 * c + 64],
                                                       rhs=sc[:, 20:24], start=True, stop=True), [cm.b, sc.b], [pM.b])
                    yield
                    fw.op("dve", lambda e: e.tensor_copy(out=sc[:, 24:28], in_=pM[:, 0:4]), [pM.b], [sc.b])
                    fw.op("dve", lambda e: e.tensor_scalar(out=sc[:, 28:32], in0=pM[:, 0:4], scalar1=-1.0, scalar2=None,
                                                           op0=ALU.mult), [pM.b], [sc.b])
                    fw.op("act", lambda e: e.activation(out=sc[:, 32:36], in_=pM[:, 0:4], func=AF.Exp, scale=-1.0),
                          [pM.b], [sc.b])
                    yield
                    fw.op("dve", lambda e: e.tensor_tensor(out=sc[:, 36:40], in0=sc[:, 24:28], in1=pM[:, 4:8], op=ALU.subtract),
                          [sc.b, pM.b], [sc.b])
                    fw.op("act", lambda e: e.activation(out=sc[:, 36:40], in_=sc[:, 36:40], func=AF.Exp), [sc.b], [sc.b])
                    fw.op("act", lambda e: e.activation(out=cd[:].rearrange("p c h -> p (c h)"), in_=pM[0:64, 8:16],
                                                        func=AF.Exp, scale=-1.0), [pM.b], [cd.b])
                    yield
                    for h in range(4):
                        hs = slice(h * 64, (h + 1) * 64)
                        ks = slice(256 + h * 64, 256 + (h + 1) * 64)
                        vs = slice(512 + h * 64, 512 + (h + 1) * 64)
                        e1 = "dve" if h % 2 == 0 else "pool"
                        e2 = "pool" if h % 2 == 0 else "dve"
                        fw.op(e1, lambda e: e.tensor_scalar(out=qn[:, hs], in0=qkv[:, hs], scalar1=sc[:, 8 + h:9 + h],
                                                            scalar2=0.125, op0=ALU.mult, op1=ALU.mult), [qkv.b, sc.b], [qn.b])
                        fw.op(e2, lambda e: e.tensor_scalar(out=kn[:, hs], in0=qkv[:, ks], scalar1=sc[:, 12 + h:13 + h],
                                                            scalar2=None, op0=ALU.mult), [qkv.b, sc.b], [kn.b])
                        fw.op(e1, lambda e: e.tensor_scalar(out=kb[:, hs], in0=kn[:, hs], scalar1=sc[:, 16 + h:17 + h],
                                                            scalar2=None, op0=ALU.mult), [kn.b, sc.b], [kb.b])
                        fw.op(e2, lambda e: e.tensor_scalar(out=kbd[:, hs], in0=kb[:, hs], scalar1=sc[:, 32 + h:33 + h],
                                                            scalar2=None, op0=ALU.mult), [kb.b, sc.b], [kbd.b])
                        fw.op(e1, lambda e: e.tensor_scalar(out=vb[:, hs], in0=qkv[:, vs], scalar1=sc[:, 16 + h:17 + h],
                                                            scalar2=None, op0=ALU.mult), [qkv.b, sc.b], [vb.b])
                        fw.op(e2, lambda e: e.tensor_scalar(out=qd[:, hs], in0=qn[:, hs], scalar1=sc[:, 32 + h:33 + h],
                                                            scalar2=None, op0=ALU.mult), [qn.b, sc.b], [qd.b])
                        fw.op(e1, lambda e: e.tensor_scalar(out=kt[:, hs], in0=kn[:, hs], scalar1=sc[:, 36 + h:37 + h],
                                                            scalar2=None, op0=ALU.mult), [kn.b, sc.b], [kt.b])
                        yield
                def head_gen(h, t):
                    hs = slice(h * 64, (h + 1) * 64)
                    d_ = PA[t % 2]
                    sc = d_["sc"]; qn = d_["qn"]; kn = d_["kn"]; kb = d_["kb"]; kbd = d_["kbd"]; vb = d_["vb"]
                    qd = d_["qd"]; kt = d_["kt"]; cd = d_["cd"]; o = d_["o"]
                    pH = ph[h]; TT = TTh[h]; gd = gdh[h]; dec = dech[h]; qkm = qkmh[h]
                    u = uh[h]; wT = wTh[h]; vnew = vnewh[h]
                    Mr = Mrh[h]; Nr = Nrh[h]; Qr = Qrh[h]
                    for j, src in enumerate((kn, kb, qn, qd)):
                        fw.op("pe", lambda e: e.transpose(out=pH[0:64, j * 128:(j + 1) * 128], in_=src[:, hs], identity=ident),
                              [src.b, cm.b], [pH.b])
                    fw.op("dve", lambda e: e.tensor_scalar(out=gd[:], in0=ident, scalar1=sc[:, 24 + h:25 + h], scalar2=None,
                                                           op0=ALU.mult), [cm.b, sc.b], [gd.b])
                    yield
                    fw.op("act", lambda e: e.copy(out=TT[:].rearrange("p a b -> p (a b)"), in_=pH[0:64, :]),
                          [pH.b], [TT.b])
                    kT_ = TT[:, 0, :]; kbT = TT[:, 1, :]; qT_ = TT[:, 2, :]; qdT = TT[:, 3, :]
                    yield
                    for j, cmask in enumerate((C_DTS, C_DSTS, C_DSTI)):
                        fw.op("pe", lambda e: e.matmul(pH[:, j * 128:(j + 1) * 128], lhsT=cm[:, C_ONES, :], rhs=gd[:],
                                                       start=True, stop=False), [cm.b, gd.b], [pH.b])
                        fw.op("pe", lambda e: e.matmul(pH[:, j * 128:(j + 1) * 128], lhsT=ident, rhs=cm[:, cmask, :],
                                                       start=False, stop=True), [cm.b], [pH.b])
                    yield
                    fw.op("act", lambda e: e.activation(out=dec[0][:], in_=pH[:, 0:128], func=AF.Exp,
                                                        bias=sc[:, 28 + h:29 + h], scale=1.0), [pH.b, sc.b], [dec[0].b])
                    fw.op("act", lambda e: e.activation(out=dec[1][:], in_=pH[:, 128:256], func=AF.Exp,
                                                        bias=sc[:, 24 + h:25 + h], scale=-1.0), [pH.b, sc.b], [dec[1].b])
                    fw.op("act", lambda e: e.activation(out=dec[2][:], in_=pH[:, 256:384], func=AF.Exp,
                                                        bias=sc[:, 24 + h:25 + h], scale=-1.0), [pH.b, sc.b], [dec[2].b])
                    yield
                    fw.op("pe", lambda e: e.matmul(pH[:, 0:128], lhsT=kbT, rhs=kT_, start=True, stop=True), [TT.b], [pH.b])
                    fw.op("pe", lambda e: e.matmul(pH[:, 128:256], lhsT=kT_, rhs=kbT, start=True, stop=True), [TT.b], [pH.b])
                    fw.op("pe", lambda e: e.matmul(pH[:, 256:384], lhsT=kT_, rhs=qT_, start=True, stop=True), [TT.b], [pH.b])
                    yield
                    M = Mr.next(); N = Nr.next(); Q = Qr.next()
                    fw.op("dve", lambda e: e.scalar_tensor_tensor(out=M[:], in0=pH[:, 0:128], scalar=-1.0, in1=dec[0][:],
                                                                  op0=ALU.mult, op1=ALU.mult), [pH.b, dec[0].b], [M.b])
                    fw.op("dve", lambda e: e.scalar_tensor_tensor(out=N[:], in0=pH[:, 128:256], scalar=-1.0, in1=dec[1][:],
                                                                  op0=ALU.mult, op1=ALU.mult), [pH.b, dec[1].b], [N.b])
                    fw.op("dve", lambda e: e.tensor_tensor(out=qkm[:], in0=pH[:, 256:384], in1=dec[2][:], op=ALU.mult),
                          [pH.b, dec[2].b], [qkm.b])
                    fw.op("pool", lambda e: e.tensor_tensor(out=Q[:], in0=N[:], in1=ident, op=ALU.add), [N.b, cm.b], [Q.b])
                    yield
                    for lev in range(5):
                        M2 = Mr.next()
                        fw.op("pe", lambda e: e.matmul(pH[:, 0:128], lhsT=N[:], rhs=M[:], start=True, stop=True),
                              [N.b, M.b], [pH.b])
                        if lev < 4:
                            N2 = Nr.next()
                            fw.op("pe", lambda e: e.matmul(pH[:, 128:256], lhsT=M[:], rhs=N[:], start=True, stop=True),
                                  [N.b, M.b], [pH.b])
                        yield
                        fw.op("act", lambda e: e.copy(out=M2[:], in_=pH[:, 0:128]), [pH.b], [M2.b])
                        if lev < 4:
                            fw.op("act", lambda e: e.copy(out=N2[:], in_=pH[:, 128:256]), [pH.b], [N2.b])
                        yield
                        fw.op("pe", lambda e: e.matmul(pH[:, 256:384], lhsT=M2[:], rhs=Q[:], start=True, stop=True),
                              [M2.b, Q.b], [pH.b])
                        yield
                        Q2 = Qr.next()
                        fw.op("dve", lambda e: e.tensor_tensor(out=Q2[:], in0=pH[:, 256:384], in1=Q[:], op=ALU.add),
                              [pH.b, Q.b], [Q2.b])
                        yield
                        M = M2
                        if lev < 4:
                            N = N2
                        Q = Q2
                    fw.op("pe", lambda e: e.matmul(pH[:, 0:64], lhsT=Q[:], rhs=vb[:, hs], start=True, stop=True),
                          [Q.b, vb.b], [pH.b])
                    fw.op("pe", lambda e: e.matmul(pH[0:64, 64:192], lhsT=kbd[:, hs], rhs=Q[:], start=True, stop=True),
                          [Q.b, kbd.b], [pH.b])
                    yield
                    fw.op("act", lambda e: e.copy(out=u[:], in_=pH[:, 0:64]), [pH.b], [u.b])
                    fw.op("dve", lambda e: e.tensor_copy(out=wT[:], in_=pH[0:64, 64:192]), [pH.b], [wT.b])
                    yield
                    for c in range(2):
                        r = slice(64 * c, 64 * c + 64)
                        fw.op("pe", lambda e: e.matmul(pH[r, 192:256], lhsT=wT[:, r], rhs=S[h][:], start=True, stop=True),
                              [wT.b, S[h].b], [pH.b])
                        fw.op("pe", lambda e: e.matmul(pH[r, 256:320], lhsT=qdT[:, r], rhs=S[h][:], start=True, stop=False),
                              [TT.b, S[h].b], [pH.b])
                        yield
                        fw.op("dve", lambda e: e.tensor_tensor(out=vnew[r, :], in0=u[r, :], in1=pH[r, 192:256], op=ALU.subtract),
                              [u.b, pH.b], [vnew.b])
                        yield
                        fw.op("pe", lambda e: e.matmul(pH[r, 256:320], lhsT=qkm[:, r], rhs=vnew[:, :], start=False, stop=True),
                              [qkm.b, vnew.b], [pH.b])
                        fw.op("pe", lambda e: e.matmul(pH[0:64, 320:384], lhsT=kt[r, hs], rhs=vnew[r, :], start=True, stop=True),
                              [kt.b, vnew.b], [pH.b])
                        yield
                        fw.op("act", lambda e: e.copy(out=o[r, hs], in_=pH[r, 256:320]), [pH.b], [o.b])
                        fw.op("dve", lambda e: e.scalar_tensor_tensor(out=S[h][:], in0=S[h][:], scalar=cd[:, c, h:h + 1],
                                                                      in1=pH[0:64, 320:384], op0=ALU.mult, op1=ALU.add),
                              [S[h].b, cd.b, pH.b], [S[h].b])
                        yield


                def epi_gen(t):
                    d_ = PA[t % 2]
                    for _ in head_norm_gate_gen(d_["o"], d_["pj"][:, O_Z:O_Z + 256], [d_["pj"].b], gain_bc, tmp, t, 0):
                        yield

                run_interleaved([pre_gen(0)])
                for t in range(NT):
                    gens = [head_gen(h, t) for h in range(4)]
                    if t >= 1:
                        gens.append(epi_gen(t - 1))
                    if t + 1 < NT:
                        gens.append(pre_gen(t + 1))
                    run_interleaved(gens)
                run_interleaved([epi_gen(NT - 1)])
                fw.barrier()

        def attention(sb, ps, qT, kT, Vc, Dk, scale, bias_tb, negmask, col0):
            def mkslot(i):
                return dict(pS=[ps(f"paS{i}_0"), ps(f"paS{i}_1")], po=ps(f"paO{i}"), pTt=ps(f"paT{i}"),
                            P=[sb(f"P{i}_{k}", [128, 512], BF16) for k in range(2)], oT=sb(f"oT{i}", [65, 512]),
                            rinv=sb(f"rinv{i}", [128, 4]))
            slots = [mkslot(i) for i in range(2)]
            obs = [sb(f"obq{i}", [128, 4, 256], BF16) for i in range(2)]

            def chain(qb, h, sl):
                d_ = slots[sl]
                po = d_["po"]; pTt = d_["pTt"]; o1 = d_["oT"]; ri = d_["rinv"]
                ob = obs[qb % 2]
                nk = 4 * qb + 4

                def emit_S(j):
                    jl = j - 4 * qb
                    c0 = 0 if jl < 0 else jl * 128
                    pS1 = d_["pS"][j % 2]
                    fw.op("pe", lambda e: e.matmul(pS1[:, c0:512], lhsT=kT[0:Dk, h, j * 128:(j + 1) * 128],
                                                   rhs=qT[0:Dk, h, qb * 512 + c0:(qb + 1) * 512],
                                                   start=True, stop=(jl < 0)), [kT.b, qT.b], [pS1.b])
                    if jl >= 0:
                        fw.op("pe", lambda e: e.matmul(pS1[:, c0:c0 + 128], lhsT=identb, rhs=cmb[:, negmask, :],
                                                       start=False, stop=True), [cmb.b], [pS1.b])
                    return pS1, c0

                pend = emit_S(0)
                yield
                for j in range(nk):
                    pS1, c0 = pend
                    if j + 1 < nk:
                        pend = emit_S(j + 1)
                    P = d_["P"][j % 2]
                    if bias_tb is None:
                        fw.op("act", lambda e: e.activation(out=P[:, c0:512], in_=pS1[:, c0:512], func=AF.Exp, scale=scale),
                              [pS1.b], [P.b])
                    else:
                        fw.op("act", lambda e: e.activation(out=P[:, c0:512], in_=pS1[:, c0:512], func=AF.Exp, scale=scale,
                                                            bias=bias_tb[:, j, h:h + 1]), [pS1.b, bias_tb.b], [P.b])
                    fw.op("pe", lambda e: e.matmul(po[0:65, c0:512], lhsT=Vc[:, j, h, :], rhs=P[:, c0:512],
                                                   start=(j == 0), stop=(j == nk - 1)), [Vc.b, P.b], [po.b])
                    yield
                fw.op("dve", lambda e: e.tensor_copy(out=o1[:], in_=po[0:65, :]), [po.b], [o1.b])
                yield
                for jq in range(4):
                    fw.op("pe", lambda e: e.transpose(out=pTt[:, jq * 66:jq * 66 + 65], in_=o1[:, jq * 128:(jq + 1) * 128],
                                                      identity=ident[0:65, 0:65]), [o1.b, cm.b], [pTt.b])
                yield
                fw.op("dve", lambda e: e.reciprocal(out=ri[:], in_=pTt[:, 0:264].rearrange("p (j c) -> p j c", c=66)[:, :, 64]),
                      [pTt.b], [ri.b])
                yield
                for jq in range(4):
                    fw.op("dve" if jq % 2 else "act",
                          (lambda e: e.tensor_scalar(out=ob[:, jq, h * 64:(h + 1) * 64], in0=pTt[:, jq * 66:jq * 66 + 64],
                                                     scalar1=ri[:, jq:jq + 1], scalar2=None, op0=ALU.mult)) if jq % 2 else
                          (lambda e: e.activation(out=ob[:, jq, h * 64:(h + 1) * 64], in_=pTt[:, jq * 66:jq * 66 + 64],
                                                  func=AF.Copy, scale=ri[:, jq:jq + 1])),
                          [pTt.b, ri.b], [ob.b])
                if h == 3:
                    yield
                    fw.dma("pool", obr_d[qb * 512:(qb + 1) * 512, col0:col0 + 256].rearrange("(j p) c -> p j c", p=128), ob[:],
                           reads=[ob.b])

            run_pipelined([(lambda sl, qb=qb, h=h: chain(qb, h, sl)) for qb in range(8) for h in range(4)], 2)

        def passB(l):
            with contextlib.ExitStack() as st:
                sb, ps = mk_alloc(st)
                colv = col_vectors(st, l)
                wq = sb("wq", [128, 2, 384], BF16)
                wkv = sb("wkv", [128, 512], BF16)
                stg = Ring([sb(f"stgB{i}", [128, 512]) for i in range(2)])
                for k in range(2):
                    load_cast(wq, wq[:, k, :], W["mla_w_q_up"][l, k * 128:(k + 1) * 128, :], stg, "dve",
                              colv[:, 16 + k:17 + k], [colv.b])
                load_cast(wkv, wkv[:], W["mla_w_kv_up"][l], stg, "dve", colv[:, 18:19], [colv.b])
                qT = sb("qTB", [96, 4, T], BF16)
                kT = sb("kTB", [96, 4, T], BF16)
                Vc = sb("VcB", [128, NT, 4, 65], BF16)
                fw.op("pool", lambda e: e.memset(Vc[:], 1.0), [], [Vc.b])
                with contextlib.ExitStack() as st2:
                    sb2, ps2 = mk_alloc(st2)
                    def mkslot(i):
                        return dict(pj=sb2(f"pjB{i}", [128, 416]), rp=sb2(f"rp{i}", [128, 256]), stat=sb2(f"statB{i}", [128, 4]),
                                    junk=sb2(f"junkB{i}", [128, 256]), cn=sb2(f"cn{i}", [128, 384], BF16),
                                    cnT=sb2(f"cnT{i}", [128, 3, 128], BF16), qf=sb2(f"qf{i}", [128, 4, 96], BF16),
                                    kf=sb2(f"kf{i}", [128, 4, 96], BF16), t1=sb2(f"t1{i}", [128, 4, 32]),
                                    t2=sb2(f"t2{i}", [128, 4, 32]), kp=sb2(f"kp{i}", [128, 32], BF16),
                                    t1k=sb2(f"t1k{i}", [128, 32]), t2k=sb2(f"t2k{i}", [128, 32]),
                                    pT=ps2(f"pTB{i}", [128, 1024], BF16), pq=ps2(f"pq{i}"), pkv=ps2(f"pkv{i}"))
                    slots = [mkslot(i) for i in range(2)]

                    def tile_gen(t, sl):
                            d_ = slots[sl]
                            pj = d_["pj"]; rp = d_["rp"]; stat = d_["stat"]; junk = d_["junk"]; cn = d_["cn"]; cnT = d_["cnT"]
                            qf = d_["qf"]; kf = d_["kf"]; t1 = d_["t1"]; t2 = d_["t2"]; kp = d_["kp"]; t1k = d_["t1k"]; t2k = d_["t2k"]
                            pT = d_["pT"]; pq = d_["pq"]; pkv = d_["pkv"]
                            fw.dma("sp", pj[:], proj_d[t * 128:(t + 1) * 128, O_CQ:O_CQ + 416], writes=[pj.b])
                            fw.dma("sp", rp[:], rope_d[t * 128:(t + 1) * 128, :], writes=[rp.b])
                            fw.op("act", lambda e: e.activation(out=junk[:, 0:256], in_=pj[:, 0:256], func=AF.Square,
                                                                accum_out=stat[:, 0:1]), [pj.b], [junk.b, stat.b])
                            fw.op("act", lambda e: e.activation(out=junk[:, 0:128], in_=pj[:, 256:384], func=AF.Square,
                                                                accum_out=stat[:, 1:2]), [pj.b], [junk.b, stat.b])
                            yield
                            rstd_from_ssq(stat[:, 0:1], stat[:, 2:3], 256, [stat.b])
                            rstd_from_ssq(stat[:, 1:2], stat[:, 3:4], 128, [stat.b])
                            yield
                            fw.op("dve", lambda e: e.tensor_scalar(out=cn[:, 0:256], in0=pj[:, 0:256], scalar1=stat[:, 2:3],
                                                                   scalar2=None, op0=ALU.mult), [pj.b, stat.b], [cn.b])
                            fw.op("dve", lambda e: e.tensor_scalar(out=cn[:, 256:384], in0=pj[:, 256:384], scalar1=stat[:, 3:4],
                                                                   scalar2=None, op0=ALU.mult), [pj.b, stat.b], [cn.b])
                            yield
                            for c in range(3):
                                fw.op("pe", lambda e: e.transpose(out=pT[:, c * 128:(c + 1) * 128], in_=cn[:, c * 128:(c + 1) * 128],
                                                                  identity=identb), [cn.b, cmb.b], [pT.b])
                            yield
                            fw.op("act", lambda e: e.copy(out=cnT[:].rearrange("p a b -> p (a b)"), in_=pT[:, 0:384]),
                                  [pT.b], [cnT.b])
                            yield
                            for k in range(2):
                                fw.op("pe", lambda e: e.matmul(pq[:, 0:384], lhsT=cnT[:, k, :], rhs=wq[:, k, :], start=(k == 0),
                                                               stop=(k == 1)), [cnT.b, wq.b], [pq.b])
                            fw.op("pe", lambda e: e.matmul(pkv[:, 0:512], lhsT=cnT[:, 2, :], rhs=wkv[:], start=True, stop=True),
                                  [cnT.b, wkv.b], [pkv.b])
                            yield
                            pq3 = pq[:, 0:384].rearrange("p (h d) -> p h d", h=4)
                            pkv3 = pkv[:, 0:512].rearrange("p (h d) -> p h d", h=4)
                            cos3 = rp[:, 0:128].rearrange("p (h d) -> p h d", h=4)
                            sin3 = rp[:, 128:256].rearrange("p (h d) -> p h d", h=4)
                            fw.op("act", lambda e: e.copy(out=qf[:, :, 0:64], in_=pq3[:, :, 0:64]), [pq.b], [qf.b])
                            fw.op("dve", lambda e: e.tensor_tensor(out=t1[:], in0=pq3[:, :, 64:96], in1=cos3, op=ALU.mult),
                                  [pq.b, rp.b], [t1.b])
                            fw.op("dve", lambda e: e.tensor_tensor(out=t2[:, :, 0:16], in0=pq3[:, :, 80:96], in1=sin3[:, :, 0:16],
                                                                   op=ALU.mult), [pq.b, rp.b], [t2.b])
                            fw.op("dve", lambda e: e.tensor_tensor(out=t2[:, :, 16:32], in0=pq3[:, :, 64:80], in1=sin3[:, :, 16:32],
                                                                   op=ALU.mult), [pq.b, rp.b], [t2.b])
                            yield
                            fw.op("pool", lambda e: e.tensor_tensor(out=qf[:, :, 64:96], in0=t1[:], in1=t2[:], op=ALU.add),
                                  [t1.b, t2.b], [qf.b])
                            fw.op("dve", lambda e: e.tensor_tensor(out=t1k[:], in0=pj[:, 384:416], in1=rp[:, 0:32], op=ALU.mult),
                                  [pj.b, rp.b], [t1k.b])
                            fw.op("dve", lambda e: e.tensor_tensor(out=t2k[:, 0:16], in0=pj[:, 400:416], in1=rp[:, 128:144],
                                                                   op=ALU.mult), [pj.b, rp.b], [t2k.b])
                            fw.op("dve", lambda e: e.tensor_tensor(out=t2k[:, 16:32], in0=pj[:, 384:400], in1=rp[:, 144:160],
                                                                   op=ALU.mult), [pj.b, rp.b], [t2k.b])
                            fw.op("pool", lambda e: e.tensor_tensor(out=kp[:], in0=t1k[:], in1=t2k[:], op=ALU.add),
                                  [t1k.b, t2k.b], [kp.b])
                            fw.op("act", lambda e: e.copy(out=kf[:, :, 0:64], in_=pkv3[:, :, 0:64]), [pkv.b], [kf.b])
                            for h in range(4):
                                fw.op("pool", lambda e: e.tensor_copy(out=kf[:, h, 64:96], in_=kp[:]), [kp.b], [kf.b])
                            fw.op("dve", lambda e: e.tensor_copy(out=Vc[:, t, :, 0:64], in_=pkv3[:, :, 64:128]), [pkv.b], [Vc.b])
                            yield
                            for h in range(4):
                                fw.op("pe", lambda e: e.transpose(out=pT[0:96, h * 128:(h + 1) * 128], in_=qf[:, h, :], identity=identb),
                                      [qf.b, cmb.b], [pT.b])
                                fw.op("pe", lambda e: e.transpose(out=pT[0:96, (4 + h) * 128:(5 + h) * 128], in_=kf[:, h, :],
                                                                  identity=identb), [kf.b, cmb.b], [pT.b])
                            yield
                            fw.op("act", lambda e: e.copy(out=qT[:, :, t * 128:(t + 1) * 128],
                                                          in_=pT[0:96, 0:512].rearrange("p (h n) -> p h n", h=4)), [pT.b], [qT.b])
                            fw.op("dve", lambda e: e.tensor_copy(out=kT[:, :, t * 128:(t + 1) * 128],
                                                                 in_=pT[0:96, 512:1024].rearrange("p (h n) -> p h n", h=4)),
                                  [pT.b], [kT.b])

                    run_pipelined([(lambda sl, t=t: tile_gen(t, sl)) for t in range(NT)], 2)
                    fw.barrier()
                with contextlib.ExitStack() as st3:
                    sb3, ps3 = mk_alloc(st3)
                    attention(sb3, ps3, qT, kT, Vc, 96, 96 ** -0.5, None, C_NCHUNK, 256)
                    fw.barrier()

        def passD(l):
            with contextlib.ExitStack() as st:
                sb, ps = mk_alloc(st)
                qT = sb("qTD", [67, 4, T], BF16)
                kT = sb("kTD", [67, 4, T], BF16)
                Vc = sb("VcD", [128, NT, 4, 65], BF16)
                fneg = sb("fneg", [128, NT, 4])
                fw.op("pool", lambda e: e.memset(Vc[:], 1.0), [], [Vc.b])
                with contextlib.ExitStack() as st2:
                    sb2, ps2 = mk_alloc(st2)
                    fb = bcast_load(sb2, "fbias", W["fox_f_bias"][l], 4)
                    acc = sb2("accD", [128, 4])
                    fw.op("pool", lambda e: e.memset(acc[:], 0.0), [], [acc.b])

                    def mkslot(i):
                        d_ = dict(pj=sb2(f"pjD{i}", [128, 772]), sp_=sb2(f"spD{i}", [128, 4]), f8=sb2(f"f8{i}", [128, 4]),
                                  pcb=sb2(f"pcb{i}", [128, 4], BF16), qa=sb2(f"qa{i}", [128, 4, 67], BF16),
                                  ka=sb2(f"ka{i}", [128, 4, 67], BF16), pT=ps2(f"pTD{i}", [128, 1024], BF16), pF=ps2(f"pF{i}"))
                        fw.op("pool", lambda e: e.memset(d_["ka"][:], 1.0), [], [d_["ka"].b])
                        return d_
                    slots = [mkslot(i) for i in range(2)]

                    def tile_gen(t, sl):
                            d_ = slots[sl]
                            pj = d_["pj"]; sp_ = d_["sp_"]; f8 = d_["f8"]; pcb = d_["pcb"]; qa = d_["qa"]; ka = d_["ka"]
                            pT = d_["pT"]; pF = d_["pF"]
                            fw.dma("sp", pj[:], proj_d[t * 128:(t + 1) * 128, O_FQ:O_FQ + 772], writes=[pj.b])
                            fw.op("dve", lambda e: e.tensor_tensor(out=sp_[:], in0=pj[:, 768:772], in1=fb[:], op=ALU.add),
                                  [pj.b, fb.b], [sp_.b])
                            fw.op("act", lambda e: e.activation(out=sp_[:], in_=sp_[:], func=AF.Exp, scale=-1.0), [sp_.b], [sp_.b])
                            fw.op("act", lambda e: e.activation(out=sp_[:], in_=sp_[:], func=AF.Ln, bias=1.0), [sp_.b], [sp_.b])
                            fw.op("pe", lambda e: e.matmul(pF[:, 0:4], lhsT=cm[:, C_CAUS, :], rhs=sp_[:], start=True, stop=False),
                                  [cm.b, sp_.b], [pF.b])
                            fw.op("pe", lambda e: e.matmul(pF[:, 0:4], lhsT=cm[:, C_ONES, :], rhs=acc[:], start=False, stop=True),
                                  [cm.b, acc.b], [pF.b])
                            fw.op("pool", lambda e: e.tensor_tensor(out=acc[:], in0=acc[:], in1=sp_[:], op=ALU.add),
                                  [acc.b, sp_.b], [acc.b])
                            yield
                            fw.op("act", lambda e: e.copy(out=fneg[:, t, :], in_=pF[:, 0:4]), [pF.b], [fneg.b])
                            fw.op("dve", lambda e: e.tensor_scalar(out=f8[:], in0=pF[:, 0:4], scalar1=-8.0, scalar2=None, op0=ALU.mult),
                                  [pF.b], [f8.b])
                            yield
                            fw.op("act", lambda e: e.copy(out=qa[:, :, 0:64], in_=pj[:, 0:256].rearrange("p (h d) -> p h d", h=4)),
                                  [pj.b], [qa.b])
                            fw.op("pool", lambda e: e.tensor_copy(out=ka[:, :, 0:64], in_=pj[:, 256:512].rearrange("p (h d) -> p h d", h=4)),
                                  [pj.b], [ka.b])
                            fw.op("pool", lambda e: e.tensor_copy(out=Vc[:, t, :, 0:64], in_=pj[:, 512:768].rearrange("p (h d) -> p h d", h=4)),
                                  [pj.b], [Vc.b])
                            yield
                            for i in range(3):
                                fw.op("dve", lambda e: e.tensor_copy(out=pcb[:], in_=f8[:]), [f8.b], [pcb.b])
                                fw.op("dve", lambda e: e.tensor_copy(out=qa[:, :, 64 + i], in_=pcb[:]), [pcb.b], [qa.b])
                                if i < 2:
                                    fw.op("dve", lambda e: e.tensor_tensor(out=f8[:], in0=f8[:], in1=pcb[:], op=ALU.subtract),
                                          [f8.b, pcb.b], [f8.b])
                            yield
                            for h in range(4):
                                fw.op("pe", lambda e: e.transpose(out=pT[0:67, h * 128:(h + 1) * 128], in_=qa[:, h, :], identity=identb),
                                      [qa.b, cmb.b], [pT.b])
                                fw.op("pe", lambda e: e.transpose(out=pT[0:67, (4 + h) * 128:(5 + h) * 128], in_=ka[:, h, :],
                                                                  identity=identb), [ka.b, cmb.b], [pT.b])
                            yield
                            fw.op("act", lambda e: e.copy(out=qT[:, :, t * 128:(t + 1) * 128],
                                                          in_=pT[0:67, 0:512].rearrange("p (h n) -> p h n", h=4)), [pT.b], [qT.b])
                            fw.op("dve", lambda e: e.tensor_copy(out=kT[:, :, t * 128:(t + 1) * 128],
                                                                 in_=pT[0:67, 512:1024].rearrange("p (h n) -> p h n", h=4)),
                                  [pT.b], [kT.b])

                    run_pipelined([(lambda sl, t=t: tile_gen(t, sl)) for t in range(NT)], 2)
                    fw.barrier()
                with contextlib.ExitStack() as st3:
                    sb3, ps3 = mk_alloc(st3)
                    attention(sb3, ps3, qT, kT, Vc, 67, 0.125, fneg, C_NCAUS, 768)
                    fw.barrier()

        def passC(l):
            with contextlib.ExitStack() as st:
                sb, ps = mk_alloc(st)
                gain_bc = bcast_load(sb, "gla_gain", W["gla_norm"][l], 64, 4)
                gbias = bcast_load(sb, "gla_gb", W["gla_gate_bias"][l], 128)
                wgu = sb("wgu", [16, 128])
                fw.dma("sp", wgu[:], W["gla_w_gate_up"][l], writes=[wgu.b])
                bdm = cv[:, 4:260]
                mask4 = sb("mask4", [128, 4, 128])
                for h in range(4):
                    fw.op("pool", lambda e: e.tensor_copy(out=mask4[:, h, :], in_=cm[:, C_BLKI, :]), [cm.b], [mask4.b])
                def mkslotC(i):
                    return dict(pj=sb(f"pjC{i}", [128, 784]), lrT=sb(f"lrT{i}", [16, 128]), la=sb(f"la{i}", [128, 128]),
                                bc=sb(f"bcum{i}", [128, 128]), eb=sb(f"eb{i}", [128, 128]), enb=sb(f"enb{i}", [128, 128]),
                                et=sb(f"et{i}", [128, 128]), qi=sb(f"qi{i}", [128, 128], BF16), ki=sb(f"ki{i}", [128, 128], BF16),
                                ktl=sb(f"ktl{i}", [128, 128], BF16), vbf=sb(f"vbf{i}", [128, 256], BF16),
                                kiT=sb(f"kiT{i}", [128, 128], BF16), qiT=sb(f"qiT{i}", [128, 128], BF16),
                                qm=sb(f"qm{i}", [128, 4, 128], BF16), at=sb(f"at{i}", [128, 4, 128], BF16),
                                cdc=sb(f"cdc{i}", [128, 2]), kvm=[sb(f"kvm{i}_{c}", [128, 256]) for c in range(2)],
                                o=sb(f"oC{i}", [128, 256]), oi=sb(f"oiC{i}", [128, 256]))
                PC = [mkslotC(i) for i in range(2)]
                S = sb("SC", [128, 256]); Sb = sb("SCb", [128, 256], BF16)
                tmp = (sb("sqoC", [128, 256]), sb("st4C", [128, 8]), sb("sgC", [128, 256]), sb("obC", [128, 256], BF16))
                p1 = ps("pC1"); p2 = ps("pC2"); pT = ps("pTC", [128, 1024], BF16); pA = ps("pCA"); pO = ps("pCO")
                pI = ps("pCI"); pKV = ps("pCKV")
                fw.op("pool", lambda e: e.memset(S[:], 0.0), [], [S.b])
                fw.op("pool", lambda e: e.memset(Sb[:], 0.0), [], [Sb.b])

                def pre_gen(t):
                    d_ = PC[t % 2]
                    pj = d_["pj"]; lrT = d_["lrT"]; la = d_["la"]; bc = d_["bc"]; eb = d_["eb"]; enb = d_["enb"]; et = d_["et"]
                    qi = d_["qi"]; ki = d_["ki"]; ktl = d_["ktl"]; vbf = d_["vbf"]; kiT = d_["kiT"]; qiT = d_["qiT"]
                    qm = d_["qm"]; at = d_["at"]; cdc = d_["cdc"]; kvm = d_["kvm"]; oi = d_["oi"]
                    fw.dma("sp", pj[:], proj_d[t * 128:(t + 1) * 128, O_GQ:O_GQ + 784], writes=[pj.b])
                    yield
                    fw.op("pe", lambda e: e.transpose(out=p1[0:16, 0:128], in_=pj[:, 768:784], identity=ident), [pj.b, cm.b], [p1.b])
                    yield
                    fw.op("act", lambda e: e.copy(out=lrT[:], in_=p1[0:16, 0:128]), [p1.b], [lrT.b])
                    yield
                    fw.op("pe", lambda e: e.matmul(p1[:, 128:256], lhsT=lrT[:], rhs=wgu[:], start=True, stop=True),
                          [lrT.b, wgu.b], [p1.b])
                    yield
                    fw.op("dve", lambda e: e.tensor_tensor(out=la[:], in0=p1[:, 128:256], in1=gbias[:], op=ALU.add),
                          [p1.b, gbias.b], [la.b])
                    fw.op("act", lambda e: e.activation(out=la[:], in_=la[:], func=AF.Exp, scale=-1.0), [la.b], [la.b])
                    fw.op("act", lambda e: e.activation(out=la[:], in_=la[:], func=AF.Ln, bias=1.0), [la.b], [la.b])
                    fw.op("dve", lambda e: e.tensor_scalar(out=la[:], in0=la[:], scalar1=-1.0 / 16.0, scalar2=None, op0=ALU.mult),
                          [la.b], [la.b])
                    yield
                    fw.op("pe", lambda e: e.matmul(p2[:, 0:128], lhsT=cm[:, C_BLKI, :], rhs=la[:], start=True, stop=True),
                          [cm.b, la.b], [p2.b])
                    fw.op("pe", lambda e: e.matmul(p2[:, 128:256], lhsT=cm[:, C_SAME, :], rhs=la[:], start=True, stop=True),
                          [cm.b, la.b], [p2.b])
                    fw.op("pe", lambda e: e.matmul(p2[:, 256:258], lhsT=la[:], rhs=cm[:, C_SAME, 63:65], start=True, stop=True),
                          [cm.b, la.b], [p2.b])
                    yield
                    fw.op("act", lambda e: e.copy(out=bc[:], in_=p2[:, 0:128]), [p2.b], [bc.b])
                    fw.op("act", lambda e: e.activation(out=eb[:], in_=p2[:, 0:128], func=AF.Exp), [p2.b], [eb.b])
                    fw.op("act", lambda e: e.activation(out=enb[:], in_=p2[:, 0:128], func=AF.Exp, scale=-1.0), [p2.b], [enb.b])
                    fw.op("dve", lambda e: e.tensor_tensor(out=et[:], in0=p2[:, 128:256], in1=bc[:], op=ALU.subtract),
                          [p2.b, bc.b], [et.b])
                    fw.op("act", lambda e: e.activation(out=et[:], in_=et[:], func=AF.Exp), [et.b], [et.b])
                    fw.op("act", lambda e: e.activation(out=cdc[:], in_=p2[:, 256:258], func=AF.Exp), [p2.b], [cdc.b])
                    yield
                    fw.op("dve", lambda e: e.scalar_tensor_tensor(out=qi[:], in0=pj[:, 0:128], scalar=32 ** -0.5, in1=eb[:],
                                                                  op0=ALU.mult, op1=ALU.mult), [pj.b, eb.b], [qi.b])
                    fw.op("pool", lambda e: e.tensor_tensor(out=ki[:], in0=pj[:, 128:256], in1=enb[:], op=ALU.mult),
                          [pj.b, enb.b], [ki.b])
                    fw.op("pool", lambda e: e.tensor_tensor(out=ktl[:], in0=pj[:, 128:256], in1=et[:], op=ALU.mult),
                          [pj.b, et.b], [ktl.b])
                    fw.op("act", lambda e: e.copy(out=vbf[:], in_=pj[:, 256:512]), [pj.b], [vbf.b])
                    yield
                    fw.op("pe", lambda e: e.transpose(out=pT[:, 0:128], in_=qi[:], identity=identb), [qi.b, cmb.b], [pT.b])
                    fw.op("pe", lambda e: e.transpose(out=pT[:, 128:256], in_=ki[:], identity=identb), [ki.b, cmb.b], [pT.b])
                    yield
                    fw.op("act", lambda e: e.copy(out=qiT[:], in_=pT[:, 0:128]), [pT.b], [qiT.b])
                    fw.op("dve", lambda e: e.tensor_copy(out=kiT[:], in_=pT[:, 128:256]), [pT.b], [kiT.b])
                    for h in range(4):
                        fw.op("dve" if h % 2 else "act",
                              (lambda e: e.tensor_scalar(out=qm[:, h, :], in0=pT[:, 0:128], scalar1=cv[:, h:h + 1], scalar2=None,
                                                         op0=ALU.mult)) if h % 2 else
                              (lambda e: e.activation(out=qm[:, h, :], in_=pT[:, 0:128], func=AF.Copy, scale=cv[:, h:h + 1])),
                              [pT.b, cv.b], [qm.b])
                    yield
                    for h in range(4):
                        fw.op("pe", lambda e: e.matmul(pA[:, h * 128:(h + 1) * 128], lhsT=kiT[:], rhs=qm[:, h, :], start=True,
                                                       stop=True), [kiT.b, qm.b], [pA.b])
                    yield
                    fw.op("dve", lambda e: e.tensor_tensor(out=at[:].rearrange("p a b -> p (a b)"), in0=pA[:, 0:512],
                                                           in1=mask4[:].rearrange("p a b -> p (a b)"), op=ALU.mult),
                          [pA.b, mask4.b], [at.b])
                    for h in range(4):
                        fw.op("pe", lambda e: e.matmul(pO[:, h * 64:(h + 1) * 64], lhsT=at[:, h, :], rhs=vbf[:, h * 64:(h + 1) * 64],
                                                       start=True, stop=True), [at.b, vbf.b], [pO.b])
                    yield
                    fw.op("act", lambda e: e.copy(out=oi[:], in_=pO[:, 0:256]), [pO.b], [oi.b])
                    yield
                    for c in range(2):
                        r = slice(64 * c, 64 * c + 64)
                        fw.op("pe", lambda e: e.matmul(pKV[:, 0:256], lhsT=ktl[r, :], rhs=vbf[r, :], start=True, stop=True),
                              [ktl.b, vbf.b], [pKV.b])
                        fw.op("dve", lambda e: e.tensor_tensor(out=kvm[c][:], in0=pKV[:, 0:256], in1=bdm, op=ALU.mult),
                              [pKV.b, cv.b], [kvm[c].b])
                        yield

                def state_gen(t):
                    d_ = PC[t % 2]
                    qiT = d_["qiT"]; cdc = d_["cdc"]; kvm = d_["kvm"]; oi = d_["oi"]; o = d_["o"]
                    for c in range(2):
                        r = slice(64 * c, 64 * c + 64)
                        fw.op("pe", lambda e: e.matmul(pI[r, 0:256], lhsT=qiT[:, r], rhs=Sb[:], start=True, stop=True),
                              [qiT.b, Sb.b], [pI.b])
                        fw.op("dve", lambda e: e.scalar_tensor_tensor(out=S[:], in0=S[:], scalar=cdc[:, c:c + 1], in1=kvm[c][:],
                                                                       op0=ALU.mult, op1=ALU.add), [S.b, cdc.b, kvm[c].b], [S.b])
                        yield
                        fw.op("act", lambda e: e.copy(out=Sb[:], in_=S[:]), [S.b], [Sb.b])
                        fw.op("dve", lambda e: e.tensor_tensor(out=o[r, :], in0=pI[r, 0:256], in1=oi[r, :], op=ALU.add),
                              [pI.b, oi.b], [o.b])
                        yield

                def epi_gen(t):
                    d_ = PC[t % 2]
                    for _ in head_norm_gate_gen(d_["o"], d_["pj"][:, 512:768], [d_["pj"].b], gain_bc, tmp, t, 512):
                        yield

                run_interleaved([pre_gen(0)])
                for t in range(NT):
                    gens = [state_gen(t)]
                    if t >= 1:
                        gens.append(epi_gen(t - 1))
                    if t + 1 < NT:
                        gens.append(pre_gen(t + 1))
                    run_interleaved(gens)
                run_interleaved([epi_gen(NT - 1)])
                fw.barrier()

        def pass3(l, xsrc, xdst):
            with contextlib.ExitStack() as st:
                sb, ps = mk_alloc(st)
                colv = col_vectors(st, l)
                wg = [sb(f"wg{i}", [128, 8, D], BF16) for i in range(4)]
                wup = sb("wup", [128, 4, 2, D], BF16)
                wo = sb("wo", [128, 8, D], BF16)
                xin = [sb(f"x3{i}", [128, D]) for i in range(4)]
                hin = [sb(f"hT3{i}", [128, 8, 128], BF16) for i in range(4)]
                oin = [sb(f"ob3{i}", [128, D], BF16) for i in range(4)]
                stg = Ring(xin)
                n = 0
                for i in range(4):
                    for k in range(8):
                        load_cast(wg[i], wg[i][:, k, :], W["w_gate"][l, i, k * 128:(k + 1) * 128, :], stg,
                                  "act" if n % 2 else "dve", colv[:, k:k + 1], [colv.b])
                        n += 1
                for i in range(4):
                    for k in range(2):
                        fw.dma("pool", wup[:, i, k, :], W["w_branch_up"][l, i, k * 128:(k + 1) * 128, :], writes=[wup.b])
                for k in range(8):
                    fw.dma("pool", wo[:, k, :], W["w_out"][l, k * 128:(k + 1) * 128, :], writes=[wo.b])
                bg2 = sb("bg2", [1, 2, 4 * D], BF16)
                for i in range(4):
                    isl = slice(i * D, (i + 1) * D)
                    bgf = stg.next(); bgr = stg.next()
                    fw.dma("sp", bgf[0:1, :], W["b_gate"][l, i].partition_broadcast(1), writes=[bgf.b])
                    fw.op("dve", lambda e: e.tensor_copy(out=bg2[:, 0, isl], in_=bgf[0:1, :]), [bgf.b], [bg2.b])
                    fw.op("dve", lambda e: e.tensor_tensor(out=bgr[0:1, :], in0=bgf[0:1, :], in1=bg2[:, 0, isl], op=ALU.subtract),
                          [bgf.b, bg2.b], [bgr.b])
                    fw.op("dve", lambda e: e.tensor_copy(out=bg2[:, 1, isl], in_=bgr[0:1, :]), [bgr.b], [bg2.b])
                ones1 = sb("ones1", [1, 128], BF16)
                fw.op("pool", lambda e: e.memset(ones1[:], 1.0), [], [ones1.b])
                gpost = bcast_load(sb, "gpost", W["norm_mix_post"][l], D)
                pT = ps("pT3", [128, 1024], BF16)

                def mkslot(i):
                    return dict(oT=sb(f"oT3{i}", [128, 8, 128], BF16), gate=sb(f"gate{i}", [128, D], BF16), mrg=sb(f"mrg{i}", [128, D]),
                                tmpm=sb(f"tmpm{i}", [128, D]), mb=sb(f"mb{i}", [128, D], BF16), mT=sb(f"mT{i}", [128, 8, 128], BF16),
                                y=sb(f"y3{i}", [128, D]), stat=sb(f"stat3{i}", [128, 2]),
                                pg=ps(f"pg{i}"), pu=ps(f"pu{i}"), py=ps(f"py{i}"))
                slots = [mkslot(i) for i in range(2)]

                def tile_gen(t, sl):
                    d_ = slots[sl]
                    hTt = hin[t % 4]; obt = oin[t % 4]; xt = xin[t % 4]; oT = d_["oT"]; gate = d_["gate"]; mrg = d_["mrg"]
                    tmpm = d_["tmpm"]; mb = d_["mb"]; mT = d_["mT"]; y = d_["y"]; stat = d_["stat"]
                    pg = d_["pg"]; pu = d_["pu"]; py = d_["py"]
                    fw.dma("sp", hTt[:], hT_d[:, :, t * 128:(t + 1) * 128].rearrange("c p n -> p c n"), writes=[hTt.b])
                    fw.dma("sp", obt[:], obr_d[t * 128:(t + 1) * 128, :], writes=[obt.b])
                    fw.dma("sp", xt[:], xsrc[t * 128:(t + 1) * 128, :], writes=[xt.b])
                    yield
                    for c in range(8):
                        fw.op("pe", lambda e: e.transpose(out=pT[:, c * 128:(c + 1) * 128], in_=obt[:, c * 128:(c + 1) * 128],
                                                          identity=identb), [obt.b, cmb.b], [pT.b])
                    fw.op("act", lambda e: e.copy(out=oT[:].rearrange("p a b -> p (a b)"), in_=pT[:]), [pT.b], [oT.b])
                    yield
                    for i in range(4):
                        for hf in range(2):
                            cs = slice(hf * 512, (hf + 1) * 512)
                            for k in range(8):
                                fw.op("pe", lambda e: e.matmul(pg[:], lhsT=hTt[:, k, :], rhs=wg[i][:, k, cs], start=(k == 0),
                                                               stop=False), [hTt.b, wg[i].b], [pg.b])
                            fw.op("pe", lambda e: e.matmul(pg[:], lhsT=ones1[:],
                                                           rhs=bg2[:, 0, i * D + hf * 512:i * D + (hf + 1) * 512], start=False,
                                                           stop=False), [ones1.b, bg2.b], [pg.b])
                            fw.op("pe", lambda e: e.matmul(pg[:], lhsT=ones1[:],
                                                           rhs=bg2[:, 1, i * D + hf * 512:i * D + (hf + 1) * 512], start=False,
                                                           stop=True), [ones1.b, bg2.b], [pg.b])
                            for k in range(2):
                                fw.op("pe", lambda e: e.matmul(pu[:], lhsT=oT[:, 2 * i + k, :], rhs=wup[:, i, k, cs],
                                                               start=(k == 0), stop=(k == 1)), [oT.b, wup.b], [pu.b])
                            yield
                            fw.op("act", lambda e: e.activation(out=gate[:, cs], in_=pg[:], func=AF.Sigmoid),
                                  [pg.b], [gate.b])
                            if i == 0:
                                fw.op("dve", lambda e: e.tensor_tensor(out=mrg[:, cs], in0=pu[:], in1=gate[:, cs], op=ALU.mult),
                                      [pu.b, gate.b], [mrg.b])
                            else:
                                fw.op("dve", lambda e: e.tensor_tensor(out=tmpm[:, cs], in0=pu[:], in1=gate[:, cs], op=ALU.mult),
                                      [pu.b, gate.b], [tmpm.b])
                                if i < 3:
                                    fw.op("pool", lambda e: e.tensor_tensor(out=mrg[:, cs], in0=mrg[:, cs], in1=tmpm[:, cs],
                                                                            op=ALU.add), [mrg.b, tmpm.b], [mrg.b])
                                else:
                                    fw.op("pool", lambda e: e.tensor_tensor(out=mb[:, cs], in0=mrg[:, cs], in1=tmpm[:, cs],
                                                                            op=ALU.add), [mrg.b, tmpm.b], [mb.b])
                    yield
                    for c in range(8):
                        fw.op("pe", lambda e: e.transpose(out=pT[:, c * 128:(c + 1) * 128], in_=mb[:, c * 128:(c + 1) * 128],
                                                          identity=identb), [mb.b, cmb.b], [pT.b])
                    fw.op("act", lambda e: e.copy(out=mT[:].rearrange("p a b -> p (a b)"), in_=pT[:]), [pT.b], [mT.b])
                    yield
                    for hf in range(2):
                        cs = slice(hf * 512, (hf + 1) * 512)
                        for k in range(8):
                            fw.op("pe", lambda e: e.matmul(py[:], lhsT=mT[:, k, :], rhs=wo[:, k, cs], start=(k == 0),
                                                           stop=(k == 7)), [mT.b, wo.b], [py.b])
                        yield
                        fw.op("act" if hf else "dve",
                              (lambda e: e.copy(out=y[:, cs], in_=py[:])) if hf else
                              (lambda e: e.tensor_copy(out=y[:, cs], in_=py[:])), [py.b], [y.b])
                    yield
                    for _ in post_norm_residual(y, xt, gpost, stat, mb, xdst, t):
                        yield

                run_pipelined([(lambda sl, t=t: tile_gen(t, sl)) for t in range(NT)], 2)
                fw.barrier()

        def post_norm_residual(y, xt, gpost, stat, junk, xdst, t):
            fw.op("act", lambda e: e.activation(out=junk[:], in_=y[:], func=AF.Square, accum_out=stat[:, 0:1]),
                  [y.b], [junk.b, stat.b])
            yield
            rstd_from_ssq(stat[:, 0:1], stat[:, 1:2], D, [stat.b])
            yield
            fw.op("dve", lambda e: e.scalar_tensor_tensor(out=y[:], in0=y[:], scalar=stat[:, 1:2], in1=gpost[:], op0=ALU.mult,
                                                          op1=ALU.mult), [y.b, stat.b, gpost.b], [y.b])
            yield
            fw.op("pool", lambda e: e.tensor_tensor(out=xt[:], in0=y[:], in1=xt[:], op=ALU.add), [y.b, xt.b], [xt.b])
            fw.dma("pool", xdst[t * 128:(t + 1) * 128, :], xt[:], reads=[xt.b])

        def norm_gen(xt, xs, hTt, pT, stat):
            fw.op("act", lambda e: e.activation(out=xs[:], in_=xt[:], func=AF.Square, accum_out=stat[:, 0:1]),
                  [xt.b], [xs.b, stat.b])
            yield
            rstd_from_ssq(stat[:, 0:1], stat[:, 1:2], D, [stat.b])
            yield
            fw.op("dve", lambda e: e.tensor_scalar(out=xs[:], in0=xt[:], scalar1=stat[:, 1:2], scalar2=None,
                                                   op0=ALU.mult), [xt.b, stat.b], [xs.b])
            yield
            for c in range(8):
                fw.op("pe", lambda e: e.transpose(out=pT[:, c * 128:(c + 1) * 128], in_=xs[:, c * 128:(c + 1) * 128],
                                                  identity=identb), [xs.b, cmb.b], [pT.b])
            yield
            fw.op("act", lambda e: e.copy(out=hTt[:].rearrange("p a b -> p (a b)"), in_=pT[:]), [pT.b], [hTt.b])
            yield

        def pass4(l, xsrc, xdst):
            with contextlib.ExitStack() as st:
                sb, ps = mk_alloc(st)
                colv = col_vectors(st, l)
                w1 = [sb(f"w1_{q}", [128, 8, D], BF16) for q in range(4)]
                w2 = sb("w2", [128, 32, D], BF16)
                G = 2
                xr = [[sb(f"x4{s_}_{j}", [128, D]) for j in range(G)] for s_ in range(2)]
                stg = Ring([xr[0][0], xr[0][1], xr[1][0], xr[1][1]])
                n = 0
                for q4 in range(4):
                    for k in range(8):
                        load_cast(w1[q4], w1[q4][:, k, :],
                                  W["w_mlp_in"][l, k * 128:(k + 1) * 128, q4 * 1024:(q4 + 1) * 1024], stg,
                                  "act" if n % 2 else "dve", colv[:, 8 + k:9 + k], [colv.b])
                        n += 1
                for k in range(32):
                    fw.dma("pool", w2[:, k, :], W["w_mlp_out"][l, k * 128:(k + 1) * 128, :], writes=[w2.b])
                gpost = bcast_load(sb, "gpost4", W["norm_mlp_post"][l], D)
                xs = sb("xs4", [128, D], BF16)
                hT = [sb(f"hT4{s_}", [128, 8, G * 128], BF16) for s_ in range(2)]
                hTt = sb("hTt4", [128, 8, 128], BF16)
                uT = sb("uT", [128, 32, G * 128], BF16)
                rl = Ring([sb(f"rl{i}", [128, G * 128]) for i in range(2)])
                yr = Ring([sb(f"y4{i}", [128, D]) for i in range(2)]); junk = sb("junk4", [128, D], BF16)
                stat = [[sb(f"stat4{s_}_{j}", [128, 2]) for j in range(G)] for s_ in range(2)]
                stat2r = Ring([sb(f"stat4b{i}", [128, 2]) for i in range(2)])
                pT = ps("pT4", [128, 1024], BF16)
                pur = Ring([ps("pu40"), ps("pu41"), ps("pu42")])
                py = [ps("py40"), ps("py41")]
                NG = NT // G

                def s1_gen(g):
                    sl = g % 2
                    for j in range(G):
                        t = g * G + j
                        xt = xr[sl][j]
                        fw.dma("sp", xt[:], xsrc[t * 128:(t + 1) * 128, :], writes=[xt.b])
                        yield
                        for _ in norm_gen(xt, xs, hTt, pT, stat[sl][j]):
                            yield
                        fw.op("pool", lambda e: e.tensor_copy(out=hT[sl][:, :, j * 128:(j + 1) * 128], in_=hTt[:]),
                              [hTt.b], [hT[sl].b])
                        yield

                def s23_gen(g):
                    sl = g % 2
                    for f in range(32):
                        pu = pur.next()
                        for k in range(8):
                            fw.op("pe", lambda e: e.matmul(pu[:, 0:G * 128], lhsT=w1[f // 8][:, k, (f % 8) * 128:(f % 8 + 1) * 128],
                                                           rhs=hT[sl][:, k, :], start=(k == 0), stop=(k == 7)),
                                  [w1[f // 8].b, hT[sl].b], [pu.b])
                        r = rl.next()
                        fw.op("act", lambda e: e.activation(out=r[:], in_=pu[:, 0:G * 128], func=AF.Relu), [pu.b], [r.b])
                        fw.op("dve" if f % 2 else "pool", lambda e: e.tensor_tensor(out=uT[:, f, :], in0=r[:], in1=r[:], op=ALU.mult),
                              [r.b], [uT.b])
                        if f % 2:
                            yield
                    for j in range(G):
                        t = g * G + j
                        y = yr.next(); stat2 = stat2r.next()
                        for hf in range(2):
                            cs = slice(hf * 512, (hf + 1) * 512)
                            for f in range(32):
                                fw.op("pe", lambda e: e.matmul(py[hf][:], lhsT=uT[:, f, j * 128:(j + 1) * 128], rhs=w2[:, f, cs],
                                                               start=(f == 0), stop=(f == 31)), [uT.b, w2.b], [py[hf].b])
                                if f % 8 == 7:
                                    yield
                            fw.op("act" if hf else "dve",
                                  (lambda e: e.copy(out=y[:, cs], in_=py[hf][:])) if hf else
                                  (lambda e: e.tensor_copy(out=y[:, cs], in_=py[hf][:])), [py[hf].b], [y.b])
                        for _ in post_norm_residual(y, xr[sl][j], gpost, stat2, junk, xdst, t):
                            yield

                run_interleaved([s1_gen(0)])
                for g in range(NG):
                    gens = [s23_gen(g)]
                    if g + 1 < NG:
                        gens.append(s1_gen(g + 1))
                    run_interleaved(gens)
                fw.barrier()

        fw.barrier()
        for l in range(n_layers):
            xsrc = x_in if l == 0 else x2_d
            xdst = out_d if l == n_layers - 1 else x2_d
            if "1" in passes:
                pass1(l, xsrc)
            if "A" in passes:
                passA(l)
            if "B" in passes:
                passB(l)
            if "C" in passes:
                passC(l)
            if "D" in passes:
                passD(l)
            if "3" in passes:
                pass3(l, xsrc, x1_d)
            if "4" in passes:
                pass4(l, x1_d, xdst)
        fw.final_wait()
        print("instructions", fw.n_inst, "waits", fw.n_wait, flush=True)
    return nc


_CACHE = {}


def kernel(**inputs):
    cm, cv, rope = host_consts()
    if "nc" not in _CACHE:
        _CACHE["nc"] = build()
    nc = _CACHE["nc"]
    x = np.ascontiguousarray(inputs["x"], dtype=np.float32)
    in_maps = []
    for c in range(8):
        m = {k: np.ascontiguousarray(v, dtype=np.float32) for k, v in inputs.items() if k != "x"}
        m["x"] = np.ascontiguousarray(x[c])
        m["cmat"] = cm
        m["cvec"] = cv
        m["rope"] = rope
        in_maps.append(m)
    res = run_bass_kernel_spmd(nc, in_maps, core_ids=list(range(8)))
    return np.stack([np.asarray(r["out"], dtype=np.float32) for r in res.results], axis=0)
```

```python
import contextlib
import os
import numpy as np
import concourse.bass as bass
import concourse.mybir as mybir
from concourse.bass_utils import run_bass_kernel_spmd

F32 = mybir.dt.float32
BF16 = mybir.dt.bfloat16
ALU = mybir.AluOpType
AF = mybir.ActivationFunctionType
AX = mybir.AxisListType

T = 4096
D = 1024
NT = 32
DEPTH = 2
EPS = 1e-6
IN_W = 3004
O_QKV, O_Z, O_B, O_A, O_CQ, O_CKV, O_KPE = 0, 768, 1024, 1028, 1032, 1288, 1416
O_GQ, O_GK, O_GV, O_GG, O_GLR = 1448, 1576, 1704, 1960, 2216
O_FQ, O_FK, O_FV, O_FF = 2232, 2488, 2744, 3000
BIG = 30000.0

(C_ID, C_CAUS, C_ONES, C_BLKI, C_SAME, C_NCAUS, C_NCHUNK, C_DTS, C_DSTS, C_DSTI,
 C_SH0, C_SH1, C_SH2, C_SH3, C_SP1, C_SP2, C_SP3) = range(17)
NCM = 17


def kstop(n):
    return int(os.environ.get('KSTOP', '99')) <= n


class Buf:
    __slots__ = ("name", "w", "r", "excl")

    def __init__(self, name="", excl=False):
        self.name = name
        self.w = None
        self.r = {}
        self.excl = excl


class FW:
    N_LANES = 8

    def __init__(self, nc, stack):
        self.nc = nc
        self.stack = stack
        self.sems = {}
        self.cnt = {}
        self.waited = {}
        self.engs = {"pe": nc.tensor, "act": nc.scalar, "dve": nc.vector,
                     "pool": nc.gpsimd, "sp": nc.sync}
        for e in self.engs:
            self._mksem(e)
            self.waited[e] = {}
        self.lane_next = {}
        for q in ("sp", "pool", "act"):
            self.lane_next[q] = 0
            for i in range(self.N_LANES):
                self._mksem(f"dma_{q}_{i}")
        self.n_inst = 0
        self.n_wait = 0

    def _mksem(self, key):
        self.sems[key] = self.stack.enter_context(self.nc.semaphore(key))
        self.cnt[key] = 0

    def _wait(self, eng, tok):
        if tok is None:
            return
        key, val = tok
        if key == eng and eng == "pe":
            return
        w = self.waited[eng]
        if w.get(key, 0) >= val:
            return
        self.engs[eng].wait_ge(self.sems[key], val)
        w[key] = val
        self.n_wait += 1

    def _deps(self, eng, reads, writes):
        for b in reads:
            self._wait(eng, b.w)
            if b.excl:
                for k, v in b.r.items():
                    if k != eng:
                        self._wait(eng, (k, v))
        for b in writes:
            self._wait(eng, b.w)
            for k, v in b.r.items():
                self._wait(eng, (k, v))

    def _mark(self, tok, reads, writes):
        k, v = tok
        for b in reads:
            if b.r.get(k, 0) < v:
                b.r[k] = v
        for b in writes:
            b.w = tok
            b.r = {}

    def op(self, eng, inst_fn, reads=(), writes=()):
        self._deps(eng, reads, writes)
        ins = inst_fn(self.engs[eng])
        self.cnt[eng] += 1
        ins.then_inc(self.sems[eng], 1)
        tok = (eng, self.cnt[eng])
        self._mark(tok, reads, writes)
        self.n_inst += 1
        return tok

    def dma(self, q, out, in_, reads=(), writes=(), **kw):
        lane = self.lane_next[q]
        self.lane_next[q] = (lane + 1) % self.N_LANES
        key = f"dma_{q}_{lane}"
        self._deps(q, reads, writes)
        self._wait(q, (key, self.cnt[key]))
        ins = self.engs[q].dma_start(out=out, in_=in_, **kw)
        self.cnt[key] += 16
        ins.then_inc(self.sems[key], 16)
        tok = (key, self.cnt[key])
        self._mark(tok, reads, writes)
        self.n_inst += 1
        return tok

    def barrier(self):
        for e in self.engs:
            for key, v in self.cnt.items():
                if v > 0:
                    self._wait(e, (key, v))

    def final_wait(self):
        for key, v in self.cnt.items():
            if v > 0:
                self._wait("sp", (key, v))


def run_interleaved(gens):
    active = list(gens)
    while active:
        nxt = []
        for g in active:
            try:
                next(g)
                nxt.append(g)
            except StopIteration:
                pass
        active = nxt


def run_pipelined(gen_fns, depth, stagger=0):
    pending = list(gen_fns)
    free = list(range(depth))
    active = []
    since_start = 10 ** 9
    while pending or active:
        while pending and free and since_start >= stagger:
            sl = free.pop(0)
            active.append((pending.pop(0)(sl), sl))
            since_start = 0
        nxt = []
        for g, sl in active:
            try:
                next(g)
                nxt.append((g, sl))
            except StopIteration:
                free.append(sl)
        active = nxt
        since_start += 1
        if not active:
            since_start = 10 ** 9


class TB:
    def __init__(self, t, name, excl=False):
        self.t = t
        self.b = Buf(name, excl)

    def __getitem__(self, k):
        return self.t[k]


class Ring:
    def __init__(self, items):
        self.items = items
        self.i = 0

    def next(self):
        it = self.items[self.i % len(self.items)]
        self.i += 1
        return it


def host_consts():
    s = np.arange(128)[:, None]
    t = np.arange(128)[None, :]
    same = (s // 64) == (t // 64)
    cm = np.zeros((NCM, 128, 128), np.float32)
    cm[C_ID] = (s == t)
    cm[C_CAUS] = (s <= t)
    cm[C_ONES] = 1.0
    cm[C_BLKI] = same & (s <= t)
    cm[C_SAME] = same
    cm[C_NCAUS] = np.where(s <= t, 0.0, -BIG)
    cm[C_NCHUNK] = np.where((s // 64) <= (t // 64), 0.0, -BIG)
    cm[C_DTS] = np.where(same & (t < s), 0.0, -BIG)
    cm[C_DSTS] = np.where(same & (s < t), 0.0, BIG)
    cm[C_DSTI] = np.where(same & (s <= t), 0.0, BIG)
    for d in range(4):
        cm[C_SH0 + d] = (t == s + d)
    for d in range(1, 4):
        cm[C_SP1 + d - 1] = (t == s + d - 128)
    p = np.arange(128)[:, None]
    cv = np.zeros((128, 4 + 256), np.float32)
    cv[:, 0:4] = (p // 32) == np.arange(4)[None, :]
    cv[:, 4:] = (p // 32) == (np.arange(256)[None, :] // 64)
    inv_freq = 1.0 / (10000.0 ** (np.arange(0, 32, 2, dtype=np.float32) / 32))
    ang = np.arange(T, dtype=np.float32)[:, None] * inv_freq[None, :]
    ang = np.concatenate([ang, ang], axis=-1).astype(np.float32)
    cos = np.cos(ang).astype(np.float32)
    sin = np.sin(ang).astype(np.float32)
    sgn = np.concatenate([-np.ones(16), np.ones(16)]).astype(np.float32)
    rope = np.concatenate([np.tile(cos, (1, 4)), np.tile(sin * sgn[None], (1, 4))], axis=1)
    return cm, cv, np.ascontiguousarray(rope.astype(np.float32))


def build(n_layers=DEPTH, debug=(), passes="1ABCD34"):
    nc = bass.Bass("TRN2", target_bir_lowering=False)

    def din(name, shape, dt=F32):
        return nc.dram_tensor(name, list(shape), dt, kind="ExternalInput").ap()

    x_in = din("x", [T, D])
    L = DEPTH
    W = {
        "norm_mix_pre": din("norm_mix_pre", [L, D]),
        "norm_mix_post": din("norm_mix_post", [L, D]),
        "norm_mlp_pre": din("norm_mlp_pre", [L, D]),
        "norm_mlp_post": din("norm_mlp_post", [L, D]),
        "w_in": din("w_in", [L, D, IN_W]),
        "dn_conv": din("dn_conv", [L, 4, 768]),
        "dn_a_log": din("dn_a_log", [L, 4]),
        "dn_dt_bias": din("dn_dt_bias", [L, 4]),
        "dn_norm": din("dn_norm", [L, 64]),
        "mla_q_norm": din("mla_q_norm", [L, 256]),
        "mla_w_q_up": din("mla_w_q_up", [L, 256, 384]),
        "mla_kv_norm": din("mla_kv_norm", [L, 128]),
        "mla_w_kv_up": din("mla_w_kv_up", [L, 128, 512]),
        "gla_w_gate_up": din("gla_w_gate_up", [L, 16, 128]),
        "gla_gate_bias": din("gla_gate_bias", [L, 128]),
        "gla_norm": din("gla_norm", [L, 64]),
        "fox_f_bias": din("fox_f_bias", [L, 4]),
        "w_branch_up": din("w_branch_up", [L, 4, 256, D]),
        "w_gate": din("w_gate", [L, 4, D, D]),
        "b_gate": din("b_gate", [L, 4, D]),
        "w_out": din("w_out", [L, D, D]),
        "w_mlp_in": din("w_mlp_in", [L, D, 4 * D]),
        "w_mlp_out": din("w_mlp_out", [L, 4 * D, D]),
    }
    cmat_d = din("cmat", [NCM, 128, 128])
    cvec_d = din("cvec", [128, 260])
    rope_d = din("rope", [T, 256])
    out_d = nc.dram_tensor("out", [T, D], F32, kind="ExternalOutput").ap()

    def dscr(name, shape, dt=F32):
        kind = "ExternalOutput" if name in debug else "Internal"
        return nc.dram_tensor(name, list(shape), dt, kind=kind).ap()

    x1_d = dscr("x1s", [T, D])
    x2_d = dscr("x2s", [T, D])
    hT_d = dscr("hTs", [8, 128, T], BF16)
    proj_d = dscr("projs", [T, IN_W])
    obr_d = dscr("obrs", [T, D], BF16)

    with contextlib.ExitStack() as gst:
        fw = FW(nc, gst)

        uid = [0]

        def mk_alloc(st):
            def sb(name, shape, dt=F32):
                uid[0] += 1
                name = f"{name}_{uid[0]}"
                return TB(st.enter_context(nc.sbuf_tensor(name, list(shape), dt)), name)

            def ps(name, shape=(128, 512), dt=F32):
                uid[0] += 1
                name = f"{name}_{uid[0]}"
                return TB(st.enter_context(nc.psum_tensor(name, list(shape), dt)), name, True)
            return sb, ps

        gsb, _ = mk_alloc(gst)
        cm = gsb("cm", [128, NCM, 128])
        cmb = gsb("cmb", [128, NCM, 128], BF16)
        cv = gsb("cv", [128, 260])
        fw.dma("sp", cm[:], cmat_d.rearrange("n p f -> p n f"), writes=[cm.b])
        fw.dma("sp", cv[:], cvec_d, writes=[cv.b])
        fw.op("dve", lambda e: e.tensor_copy(out=cmb[:], in_=cm[:]), [cm.b], [cmb.b])
        ident = cm[:, C_ID, :]
        identb = cmb[:, C_ID, :]

        def rstd_from_ssq(ssq_ap, out_ap, n, bufs, eps=EPS, mul=None):
            m = (1.0 / n) if mul is None else mul
            fw.op("dve", lambda e: e.tensor_scalar(out=out_ap, in0=ssq_ap, scalar1=m, scalar2=eps,
                                                   op0=ALU.mult, op1=ALU.add), bufs, bufs)
            fw.op("act", lambda e: e.activation(out=out_ap, in_=out_ap, func=AF.Sqrt), bufs, bufs)
            fw.op("dve", lambda e: e.reciprocal(out=out_ap, in_=out_ap), bufs, bufs)

        def load_cast(dst_tb, dst_ap, src_ap, stg_ring, eng, scale_ap=None, scale_bufs=()):
            stg = stg_ring.next()
            n = src_ap.shape[-1]
            fw.dma("sp", stg[:, 0:n], src_ap, writes=[stg.b])
            if scale_ap is None:
                if eng == "act":
                    fw.op("act", lambda e: e.copy(out=dst_ap, in_=stg[:, 0:n]), [stg.b], [dst_tb.b])
                else:
                    fw.op(eng, lambda e: e.tensor_copy(out=dst_ap, in_=stg[:, 0:n]), [stg.b], [dst_tb.b])
            else:
                if eng == "act":
                    fw.op("act", lambda e: e.activation(out=dst_ap, in_=stg[:, 0:n], func=AF.Copy, scale=scale_ap),
                          [stg.b] + list(scale_bufs), [dst_tb.b])
                else:
                    fw.op(eng, lambda e: e.tensor_scalar(out=dst_ap, in0=stg[:, 0:n], scalar1=scale_ap, scalar2=None,
                                                         op0=ALU.mult), [stg.b] + list(scale_bufs), [dst_tb.b])

        def col_vectors(st, l, pcv=None):
            sb, ps = mk_alloc(st)
            rows = sb("cvrows", [19, 128])
            colv = sb("colv", [128, 19])
            tst = None
            if pcv is None:
                tst = contextlib.ExitStack()
                _, ps_t = mk_alloc(tst)
                pcv = ps_t("pcv", [128, 512])
            fw.dma("sp", rows[0:8, :], W["norm_mix_pre"][l].rearrange("(c p) -> c p", p=128), writes=[rows.b])
            fw.dma("sp", rows[8:16, :], W["norm_mlp_pre"][l].rearrange("(c p) -> c p", p=128), writes=[rows.b])
            fw.dma("sp", rows[16:18, :], W["mla_q_norm"][l].rearrange("(c p) -> c p", p=128), writes=[rows.b])
            fw.dma("sp", rows[18:19, :], W["mla_kv_norm"][l].rearrange("(c p) -> c p", p=128), writes=[rows.b])
            fw.op("pe", lambda e: e.transpose(out=pcv[:, 0:19], in_=rows[:], identity=ident[0:19, 0:19]),
                  [rows.b, cm.b], [pcv.b])
            fw.op("dve", lambda e: e.tensor_copy(out=colv[:], in_=pcv[:, 0:19]), [pcv.b], [colv.b])
            if tst is not None:
                fw.barrier()
                tst.close()
            return colv

        def norm_tile_to_T(xt, xs, hTt, pT, stat, junk):
            fw.op("act", lambda e: e.activation(out=junk[:], in_=xt[:], func=AF.Square, accum_out=stat[:, 0:1]),
                  [xt.b], [junk.b, stat.b])
            rstd_from_ssq(stat[:, 0:1], stat[:, 1:2], D, [stat.b])
            fw.op("dve", lambda e: e.tensor_scalar(out=xs[:], in0=xt[:], scalar1=stat[:, 1:2], scalar2=None,
                                                   op0=ALU.mult), [xt.b, stat.b], [xs.b])
            for c in range(8):
                fw.op("pe", lambda e: e.transpose(out=pT[:, c * 128:(c + 1) * 128], in_=xs[:, c * 128:(c + 1) * 128],
                                                  identity=identb), [xs.b, cmb.b], [pT.b])
            fw.op("act", lambda e: e.copy(out=hTt[:].rearrange("p a b -> p (a b)"), in_=pT[:]), [pT.b], [hTt.b])

        def pass1(l, xsrc):
            with contextlib.ExitStack() as st:
                sb, ps = mk_alloc(st)
                pp = [ps(f"pp{i}") for i in range(6)]
                colv = col_vectors(st, l, pp[0])
                wbf = sb("w_in_bf", [128, 8, IN_W], BF16)
                stg = Ring([sb(f"stg{i}", [128, IN_W]) for i in range(2)])
                for k in range(8):
                    load_cast(wbf, wbf[:, k, :], W["w_in"][l, k * 128:(k + 1) * 128, :], stg,
                              "act" if k % 2 else "dve", colv[:, k:k + 1], [colv.b])

                def mkslot(i):
                    return dict(xt=sb(f"xt{i}", [128, D]), xs=sb(f"xs{i}", [128, D], BF16), hTt=sb(f"hTt{i}", [128, 8, 128], BF16),
                                pj=sb(f"pj{i}", [128, IN_W]), stat=sb(f"stat{i}", [128, 2]), pT=ps(f"pT1{i}", [128, 1024], BF16))
                slots = [mkslot(i) for i in range(2)]

                def tile_gen(t, sl):
                    d_ = slots[sl]
                    xt = d_["xt"]; xs = d_["xs"]; hTt = d_["hTt"]; pj = d_["pj"]; stat = d_["stat"]; pT = d_["pT"]
                    fw.dma("sp", xt[:], xsrc[t * 128:(t + 1) * 128, :], writes=[xt.b])
                    yield
                    for _ in norm_gen(xt, xs, hTt, pT, stat):
                        yield
                    fw.dma("pool", hT_d[:, :, t * 128:(t + 1) * 128].rearrange("c p n -> p c n"), hTt[:],
                           reads=[hTt.b])
                    for nb in range(6):
                        c0 = nb * 512
                        w = min(512, IN_W - c0)
                        pb = pp[3 * sl + nb % 3]
                        for k in range(8):
                            fw.op("pe", lambda e: e.matmul(pb[:, 0:w], lhsT=hTt[:, k, :], rhs=wbf[:, k, c0:c0 + w],
                                                           start=(k == 0), stop=(k == 7)),
                                  [hTt.b, wbf.b], [pb.b])
                        if nb % 2 == 0:
                            fw.op("act", lambda e: e.copy(out=pj[:, c0:c0 + w], in_=pb[:, 0:w]), [pb.b], [pj.b])
                        else:
                            fw.op("dve", lambda e: e.tensor_copy(out=pj[:, c0:c0 + w], in_=pb[:, 0:w]),
                                  [pb.b], [pj.b])
                        yield
                    fw.dma("pool", proj_d[t * 128:(t + 1) * 128, :], pj[:], reads=[pj.b])

                run_pipelined([(lambda sl, t=t: tile_gen(t, sl)) for t in range(NT)], 2)
                fw.barrier()

        def head_norm_gate_store(sb_o, o_tb, zsrc_ap, zsrc_bufs, gain_bc, tmp, t, col0):
            sq, st4, sg, ob = tmp
            fw.op("pool", lambda e: e.tensor_tensor(out=sq[:], in0=o_tb[:], in1=o_tb[:], op=ALU.mult), [o_tb.b], [sq.b])
            fw.op("dve", lambda e: e.tensor_reduce(out=st4[:, 0:4], in_=sq[:].rearrange("p (h d) -> p h d", h=4),
                                                   axis=AX.X, op=ALU.add), [sq.b], [st4.b])
            rstd_from_ssq(st4[:, 0:4], st4[:, 4:8], 64, [st4.b])
            fw.op("act", lambda e: e.activation(out=sg[:], in_=zsrc_ap, func=AF.Silu), zsrc_bufs, [sg.b])
            fw.op("pool", lambda e: e.tensor_tensor(out=sg[:], in0=sg[:], in1=gain_bc[:], op=ALU.mult),
                  [sg.b, gain_bc.b], [sg.b])
            for h in range(4):
                fw.op("dve", lambda e: e.scalar_tensor_tensor(out=ob[:, h * 64:(h + 1) * 64], in0=o_tb[:, h * 64:(h + 1) * 64],
                                                              scalar=st4[:, 4 + h:5 + h], in1=sg[:, h * 64:(h + 1) * 64],
                                                              op0=ALU.mult, op1=ALU.mult),
                      [o_tb.b, st4.b, sg.b], [ob.b])
            fw.dma("pool", obr_d[t * 128:(t + 1) * 128, col0:col0 + 256], ob[:], reads=[ob.b])

        def head_norm_gate_gen(o_tb, zsrc_ap, zsrc_bufs, gain_bc, tmp, t, col0):
            sq, st4, sg, ob = tmp
            fw.op("act", lambda e: e.activation(out=sg[:], in_=zsrc_ap, func=AF.Silu), zsrc_bufs, [sg.b])
            yield
            fw.op("pool", lambda e: e.tensor_tensor(out=sg[:], in0=sg[:], in1=gain_bc[:], op=ALU.mult),
                  [sg.b, gain_bc.b], [sg.b])
            fw.op("pool", lambda e: e.tensor_tensor(out=sq[:], in0=o_tb[:], in1=o_tb[:], op=ALU.mult), [o_tb.b], [sq.b])
            yield
            fw.op("dve", lambda e: e.tensor_reduce(out=st4[:, 0:4], in_=sq[:].rearrange("p (h d) -> p h d", h=4),
                                                   axis=AX.X, op=ALU.add), [sq.b], [st4.b])
            yield
            rstd_from_ssq(st4[:, 0:4], st4[:, 4:8], 64, [st4.b])
            yield
            for h in range(4):
                fw.op("dve", lambda e: e.scalar_tensor_tensor(out=ob[:, h * 64:(h + 1) * 64], in0=o_tb[:, h * 64:(h + 1) * 64],
                                                              scalar=st4[:, 4 + h:5 + h], in1=sg[:, h * 64:(h + 1) * 64],
                                                              op0=ALU.mult, op1=ALU.mult),
                      [o_tb.b, st4.b, sg.b], [ob.b])
            yield
            fw.dma("pool", obr_d[t * 128:(t + 1) * 128, col0:col0 + 256], ob[:], reads=[ob.b])

        def bcast_load(sb, name, src_ap, n, reps=1):
            tb = sb(name, [128, n * reps])
            for r in range(reps):
                fw.dma("sp", tb[:, r * n:(r + 1) * n], src_ap.partition_broadcast(128), writes=[tb.b])
            return tb

        def passA(l):
            with contextlib.ExitStack() as st:
                sb, ps = mk_alloc(st)
                convw = sb("convw", [128, 4, 768])
                for j in range(4):
                    fw.dma("sp", convw[:, j, :], W["dn_conv"][l, j].partition_broadcast(128), writes=[convw.b])
                gain_bc = bcast_load(sb, "dn_gain", W["dn_norm"][l], 64, 4)
                alog = bcast_load(sb, "alog", W["dn_a_log"][l], 4)
                dtb = bcast_load(sb, "dtb", W["dn_dt_bias"][l], 4)
                fw.op("act", lambda e: e.activation(out=alog[:], in_=alog[:], func=AF.Exp), [alog.b], [alog.b])
                xwr = Ring([sb(f"xw{i}", [128, 4, 768], BF16) for i in range(3)])

                def mkslotA(i):
                    return dict(pj=sb(f"pjA{i}", [128, 1032]), qkv=sb(f"qkvc{i}", [128, 768]), sq=sb(f"sqA{i}", [128, 512]),
                                sc=sb(f"scA{i}", [128, 64]), qn=sb(f"qn{i}", [128, 256]), kn=sb(f"kn{i}", [128, 256]),
                                kb=sb(f"kb{i}", [128, 256]), kbd=sb(f"kbd{i}", [128, 256]), vb=sb(f"vb{i}", [128, 256]),
                                qd=sb(f"qd{i}", [128, 256]), kt=sb(f"kt{i}", [128, 256]), cd=sb(f"cd{i}", [64, 2, 4]),
                                o=sb(f"oA{i}", [128, 256]))
                PA = [mkslotA(i) for i in range(2)]
                TTh = [sb(f"TT{h}", [64, 4, 128]) for h in range(4)]
                gdh = [sb(f"gd{h}", [128, 128]) for h in range(4)]
                dech = [[sb(f"dec{h}_{i}", [128, 128]) for i in range(3)] for h in range(4)]
                Mrh = [Ring([sb(f"M{h}_{i}", [128, 128]) for i in range(3)]) for h in range(4)]
                Nrh = [Ring([sb(f"N{h}_{i}", [128, 128]) for i in range(3)]) for h in range(4)]
                Qrh = [Ring([sb(f"Q{h}_{i}", [128, 128]) for i in range(3)]) for h in range(4)]
                qkmh = [sb(f"qkm{h}", [128, 128]) for h in range(4)]
                uh = [sb(f"u{h}", [128, 64]) for h in range(4)]
                wTh = [sb(f"wT{h}", [64, 128]) for h in range(4)]
                vnewh = [sb(f"vnew{h}", [128, 64]) for h in range(4)]
                S = [sb(f"S{h}", [64, 64]) for h in range(4)]
                tmp = (sb("sqo", [128, 256]), sb("st4", [128, 8]), sb("sg", [128, 256]), sb("ob", [128, 256], BF16))
                pc = [ps("pc0"), ps("pc1")]
                pM = ps("pM")
                ph = [ps(f"ph{h}") for h in range(4)]
                for h in range(4):
                    fw.op("pool", lambda e: e.memset(S[h][:], 0.0), [], [S[h].b])
                for h in range(4):
                    fw.op("pool", lambda e: e.memset(vnewh[h][:], 0.0), [], [vnewh[h].b])
                xw_hist = {}

                def pre_gen(t):
                    d_ = PA[t % 2]
                    pj = d_["pj"]; qkv = d_["qkv"]; sq = d_["sq"]; sc = d_["sc"]; qn = d_["qn"]; kn = d_["kn"]; kb = d_["kb"]
                    kbd = d_["kbd"]; vb = d_["vb"]; qd = d_["qd"]; kt = d_["kt"]; cd = d_["cd"]
                    xw_prev = xw_hist.get(t - 1)
                    fw.dma("sp", pj[:], proj_d[t * 128:(t + 1) * 128, 0:1032], writes=[pj.b])
                    xw = xwr.next()
                    xw_hist[t] = xw
                    yield
                    for d in range(4):
                        fw.op("pool" if d % 2 else "dve",
                              lambda e: e.tensor_tensor(out=xw[:, d, :], in0=pj[:, 0:768], in1=convw[:, 3 - d, :], op=ALU.mult),
                              [pj.b, convw.b], [xw.b])
                    yield
                    for hf in range(2):
                        cs = slice(hf * 384, (hf + 1) * 384)
                        nmm = 4 + (3 if xw_prev is not None else 0)
                        i = 0
                        for d in range(4):
                            fw.op("pe", lambda e: e.matmul(pc[hf][:, 0:384], lhsT=cmb[:, C_SH0 + d, :], rhs=xw[:, d, cs],
                                                           start=(i == 0), stop=(i == nmm - 1)), [cmb.b, xw.b], [pc[hf].b])
                            i += 1
                        if xw_prev is not None:
                            for d in range(1, 4):
                                fw.op("pe", lambda e: e.matmul(pc[hf][:, 0:384], lhsT=cmb[:, C_SP1 + d - 1, :],
                                                               rhs=xw_prev[:, d, cs], start=False, stop=(i == nmm - 1)),
                                      [cmb.b, xw_prev.b], [pc[hf].b])
                                i += 1
                        fw.op("act", lambda e: e.activation(out=qkv[:, cs], in_=pc[hf][:, 0:384], func=AF.Silu),
                              [pc[hf].b], [qkv.b])
                    yield
                    fw.op("pool", lambda e: e.tensor_tensor(out=sq[:], in0=qkv[:, 0:512], in1=qkv[:, 0:512], op=ALU.mult),
                          [qkv.b], [sq.b])
                    fw.op("dve", lambda e: e.tensor_reduce(out=sc[:, 0:8], in_=sq[:].rearrange("p (h d) -> p h d", h=8),
                                                           axis=AX.X, op=ALU.add), [sq.b], [sc.b])
                    yield
                    rstd_from_ssq(sc[:, 0:8], sc[:, 8:16], 1, [sc.b], eps=EPS, mul=1.0)
                    fw.op("act", lambda e: e.activation(out=sc[:, 16:20], in_=pj[:, O_B:O_B + 4], func=AF.Sigmoid),
                          [pj.b], [sc.b])
                    fw.op("dve", lambda e: e.tensor_tensor(out=sc[:, 40:44], in0=pj[:, O_A:O_A + 4], in1=dtb[:], op=ALU.add),
                          [pj.b, dtb.b], [sc.b])
                    fw.op("act", lambda e: e.activation(out=sc[:, 40:44], in_=sc[:, 40:44], func=AF.Exp), [sc.b], [sc.b])
                    fw.op("act", lambda e: e.activation(out=sc[:, 40:44], in_=sc[:, 40:44], func=AF.Ln, bias=1.0),
                          [sc.b], [sc.b])
                    fw.op("dve", lambda e: e.tensor_tensor(out=sc[:, 20:24], in0=sc[:, 40:44], in1=alog[:], op=ALU.mult),
                          [sc.b, alog.b], [sc.b])
                    yield
                    fw.op("pe", lambda e: e.matmul(pM[:, 0:4], lhsT=cm[:, C_BLKI, :], rhs=sc[:, 20:24], start=True, stop=True),
                          [cm.b, sc.b], [pM.b])
                    fw.op("pe", lambda e: e.matmul(pM[:, 4:8], lhsT=cm[:, C_SAME, :], rhs=sc[:, 20:24], start=True, stop=True),
                          [cm.b, sc.b], [pM.b])
                    for c in range(2):
                        fw.op("pe", lambda e: e.matmul(pM[0:64, 8 + 4 * c:12 + 4 * c], lhsT=cm[:, C_SAME, 64 * c:64 * c + 64],
                                                       rhs=sc[:, 20:24], start=True, stop=True), [cm.b, sc.b], [pM.b])
                    yield
                    fw.op("dve", lambda e: e.tensor_copy(out=sc[:, 24:28], in_=pM[:, 0:4]), [pM.b], [sc.b])
                    fw.op("dve", lambda e: e.tensor_scalar(out=sc[:, 28:32], in0=pM[:, 0:4], scalar1=-1.0, scalar2=None,
                                                           op0=ALU.mult), [pM.b], [sc.b])
                    fw.op("act", lambda e: e.activation(out=sc[:, 32:36], in_=pM[:, 0:4], func=AF.Exp, scale=-1.0),
                          [pM.b], [sc.b])
                    yield
                    fw.op("dve", lambda e: e.tensor_tensor(out=sc[:, 36:40], in0=sc[:, 24:28], in1=pM[:, 4:8], op=ALU.subtract),
                          [sc.b, pM.b], [sc.b])
                    fw.op("act", lambda e: e.activation(out=sc[:, 36:40], in_=sc[:, 36:40], func=AF.Exp), [sc.b], [sc.b])
                    fw.op("act", lambda e: e.activation(out=cd[:].rearrange("p c h -> p (c h)"), in_=pM[0:64, 8:16],
                                                        func=AF.Exp, scale=-1.0), [pM.b], [cd.b])
                    yield
                    for h in range(4):
                        hs = slice(h * 64, (h + 1) * 64)
                        ks = slice(256 + h * 64, 256 + (h + 1) * 64)
                        vs = slice(512 + h * 64, 512 + (h + 1) * 64)
                        e1 = "dve" if h % 2 == 0 else "pool"
                        e2 = "pool" if h % 2 == 0 else "dve"
                        fw.op(e1, lambda e: e.tensor_scalar(out=qn[:, hs], in0=qkv[:, hs], scalar1=sc[:, 8 + h:9 + h],
                                                            scalar2=0.125, op0=ALU.mult, op1=ALU.mult), [qkv.b, sc.b], [qn.b])
                        fw.op(e2, lambda e: e.tensor_scalar(out=kn[:, hs], in0=qkv[:, ks], scalar1=sc[:, 12 + h:13 + h],
                                                            scalar2=None, op0=ALU.mult), [qkv.b, sc.b], [kn.b])
                        fw.op(e1, lambda e: e.tensor_scalar(out=kb[:, hs], in0=kn[:, hs], scalar1=sc[:, 16 + h:17 + h],
                                                            scalar2=None, op0=ALU.mult), [kn.b, sc.b], [kb.b])
                        fw.op(e2, lambda e: e.tensor_scalar(out=kbd[:, hs], in0=kb[:, hs], scalar1=sc[:, 32 + h:33 + h],
                                                            scalar2=None, op0=ALU.mult), [kb.b, sc.b], [kbd.b])
                        fw.op(e1, lambda e: e.tensor_scalar(out=vb[:, hs], in0=qkv[:, vs], scalar1=sc[:, 16 + h:17 + h],
                                                            scalar2=None, op0=ALU.mult), [qkv.b, sc.b], [vb.b])
                        fw.op(e2, lambda e: e.tensor_scalar(out=qd[:, hs], in0=qn[:, hs], scalar1=sc[:, 32 + h:33 + h],
                                                            scalar2=None, op0=ALU.mult), [qn.b, sc.b], [qd.b])
                        fw.op(e1, lambda e: e.tensor_scalar(out=kt[:, hs], in0=kn[:, hs], scalar1=sc[:, 36 + h:37 + h],
                                                            scalar2=None, op0=ALU.mult), [kn.b, sc.b], [kt.b])
                        yield
                def head_gen(h, t):
                    hs = slice(h * 64, (h + 1) * 64)
                    d_ = PA[t % 2]
                    sc = d_["sc"]; qn = d_["qn"]; kn = d_["kn"]; kb = d_["kb"]; kbd = d_["kbd"]; vb = d_["vb"]
                    qd = d_["qd"]; kt = d_["kt"]; cd = d_["cd"]; o = d_["o"]
                    pH = ph[h]; TT = TTh[h]; gd = gdh[h]; dec = dech[h]; qkm = qkmh[h]
                    u = uh[h]; wT = wTh[h]; vnew = vnewh[h]
                    Mr = Mrh[h]; Nr = Nrh[h]; Qr = Qrh[h]
                    for j, src in enumerate((kn, kb, qn, qd)):
                        fw.op("pe", lambda e: e.transpose(out=pH[0:64, j * 128:(j + 1) * 128], in_=src[:, hs], identity=ident),
                              [src.b, cm.b], [pH.b])
                    fw.op("dve", lambda e: e.tensor_scalar(out=gd[:], in0=ident, scalar1=sc[:, 24 + h:25 + h], scalar2=None,
                                                           op0=ALU.mult), [cm.b, sc.b], [gd.b])
                    yield
                    fw.op("act", lambda e: e.copy(out=TT[:].rearrange("p a b -> p (a b)"), in_=pH[0:64, :]),
                          [pH.b], [TT.b])
                    kT_ = TT[:, 0, :]; kbT = TT[:, 1, :]; qT_ = TT[:, 2, :]; qdT = TT[:, 3, :]
                    yield
                    for j, cmask in enumerate((C_DTS, C_DSTS, C_DSTI)):
                        fw.op("pe", lambda e: e.matmul(pH[:, j * 128:(j + 1) * 128], lhsT=cm[:, C_ONES, :], rhs=gd[:],
                                                       start=True, stop=False), [cm.b, gd.b], [pH.b])
                        fw.op("pe", lambda e: e.matmul(pH[:, j * 128:(j + 1) * 128], lhsT=ident, rhs=cm[:, cmask, :],
                                                       start=False, stop=True), [cm.b], [pH.b])
                    yield
                    fw.op("act", lambda e: e.activation(out=dec[0][:], in_=pH[:, 0:128], func=AF.Exp,
                                                        bias=sc[:, 28 + h:29 + h], scale=1.0), [pH.b, sc.b], [dec[0].b])
                    fw.op("act", lambda e: e.activation(out=dec[1][:], in_=pH[:, 128:256], func=AF.Exp,
                                                        bias=sc[:, 24 + h:25 + h], scale=-1.0), [pH.b, sc.b], [dec[1].b])
                    fw.op("act", lambda e: e.activation(out=dec[2][:], in_=pH[:, 256:384], func=AF.Exp,
                                                        bias=sc[:, 24 + h:25 + h], scale=-1.0), [pH.b, sc.b], [dec[2].b])
                    yield
                    fw.op("pe", lambda e: e.matmul(pH[:, 0:128], lhsT=kbT, rhs=kT_, start=True, stop=True), [TT.b], [pH.b])
                    fw.op("pe", lambda e: e.matmul(pH[:, 128:256], lhsT=kT_, rhs=kbT, start=True, stop=True), [TT.b], [pH.b])
                    fw.op("pe", lambda e: e.matmul(pH[:, 256:384], lhsT=kT_, rhs=qT_, start=True, stop=True), [TT.b], [pH.b])
                    yield
                    M = Mr.next(); N = Nr.next(); Q = Qr.next()
                    fw.op("dve", lambda e: e.scalar_tensor_tensor(out=M[:], in0=pH[:, 0:128], scalar=-1.0, in1=dec[0][:],
                                                                  op0=ALU.mult, op1=ALU.mult), [pH.b, dec[0].b], [M.b])
                    fw.op("dve", lambda e: e.scalar_tensor_tensor(out=N[:], in0=pH[:, 128:256], scalar=-1.0, in1=dec[1][:],
                                                                  op0=ALU.mult, op1=ALU.mult), [pH.b, dec[1].b], [N.b])
                    fw.op("dve", lambda e: e.tensor_tensor(out=qkm[:], in0=pH[:, 256:384], in1=dec[2][:], op=ALU.mult),
                          [pH.b, dec[2].b], [qkm.b])
                    fw.op("pool", lambda e: e.tensor_tensor(out=Q[:], in0=N[:], in1=ident, op=ALU.add), [N.b, cm.b], [Q.b])
                    yield
                    for lev in range(5):
                        M2 = Mr.next()
                        fw.op("pe", lambda e: e.matmul(pH[:, 0:128], lhsT=N[:], rhs=M[:], start=True, stop=True),
                              [N.b, M.b], [pH.b])
                        if lev < 4:
                            N2 = Nr.next()
                            fw.op("pe", lambda e: e.matmul(pH[:, 128:256], lhsT=M[:], rhs=N[:], start=True, stop=True),
                                  [N.b, M.b], [pH.b])
                        yield
                        fw.op("act", lambda e: e.copy(out=M2[:], in_=pH[:, 0:128]), [pH.b], [M2.b])
                        if lev < 4:
                            fw.op("act", lambda e: e.copy(out=N2[:], in_=pH[:, 128:256]), [pH.b], [N2.b])
                        yield
                        fw.op("pe", lambda e: e.matmul(pH[:, 256:384], lhsT=M2[:], rhs=Q[:], start=True, stop=True),
                              [M2.b, Q.b], [pH.b])
                        yield
                        Q2 = Qr.next()
                        fw.op("dve", lambda e: e.tensor_tensor(out=Q2[:], in0=pH[:, 256:384], in1=Q[:], op=ALU.add),
                              [pH.b, Q.b], [Q2.b])
                        yield
                        M = M2
                        if lev < 4:
                            N = N2
                        Q = Q2
                    fw.op("pe", lambda e: e.matmul(pH[:, 0:64], lhsT=Q[:], rhs=vb[:, hs], start=True, stop=True),
                          [Q.b, vb.b], [pH.b])
                    fw.op("pe", lambda e: e.matmul(pH[0:64, 64:192], lhsT=kbd[:, hs], rhs=Q[:], start=True, stop=True),
                          [Q.b, kbd.b], [pH.b])
                    yield
                    fw.op("act", lambda e: e.copy(out=u[:], in_=pH[:, 0:64]), [pH.b], [u.b])
                    fw.op("dve", lambda e: e.tensor_copy(out=wT[:], in_=pH[0:64, 64:192]), [pH.b], [wT.b])
                    yield
                    for c in range(2):
                        r = slice(64 * c, 64 * c + 64)
                        fw.op("pe", lambda e: e.matmul(pH[r, 192:256], lhsT=wT[:, r], rhs=S[h][:], start=True, stop=True),
                              [wT.b, S[h].b], [pH.b])
                        fw.op("pe", lambda e: e.matmul(pH[r, 256:320], lhsT=qdT[:, r], rhs=S[h][:], start=True, stop=False),
                              [TT.b, S[h].b], [pH.b])
                        yield
                        fw.op("dve", lambda e: e.tensor_tensor(out=vnew[r, :], in0=u[r, :], in1=pH[r, 192:256], op=ALU.subtract),
                              [u.b, pH.b], [vnew.b])
                        yield
                        fw.op("pe", lambda e: e.matmul(pH[r, 256:320], lhsT=qkm[:, r], rhs=vnew[:, :], start=False, stop=True),
                              [qkm.b, vnew.b], [pH.b])
                        fw.op("pe", lambda e: e.matmul(pH[0:64, 320:384], lhsT=kt[r, hs], rhs=vnew[r, :], start=True, stop=True),
                              [kt.b, vnew.b], [pH.b])
                        yield
                        fw.op("act", lambda e: e.copy(out=o[r, hs], in_=pH[r, 256:320]), [pH.b], [o.b])
                        fw.op("dve", lambda e: e.scalar_tensor_tensor(out=S[h][:], in0=S[h][:], scalar=cd[:, c, h:h + 1],
                                                                      in1=pH[0:64, 320:384], op0=ALU.mult, op1=ALU.add),
                              [S[h].b, cd.b, pH.b], [S[h].b])
                        yield


                def epi_gen(t):
                    d_ = PA[t % 2]
                    for _ in head_norm_gate_gen(d_["o"], d_["pj"][:, O_Z:O_Z + 256], [d_["pj"].b], gain_bc, tmp, t, 0):
                        yield

                run_interleaved([pre_gen(0)])
                for t in range(NT):
                    gens = [head_gen(h, t) for h in range(4)]
                    if t >= 1:
                        gens.append(epi_gen(t - 1))
                    if t + 1 < NT:
                        gens.append(pre_gen(t + 1))
                    run_interleaved(gens)
                run_interleaved([epi_gen(NT - 1)])
                fw.barrier()

        def attention(sb, ps, qT, kT, Vc, Dk, scale, bias_tb, negmask, col0):
            def mkslot(i):
                return dict(pS=[ps(f"paS{i}_0"), ps(f"paS{i}_1")], po=ps(f"paO{i}"), pTt=ps(f"paT{i}"),
                            P=[sb(f"P{i}_{k}", [128, 512], BF16) for k in range(2)], oT=sb(f"oT{i}", [65, 512]),
                            rinv=sb(f"rinv{i}", [128, 4]))
            slots = [mkslot(i) for i in range(2)]
            obs = [sb(f"obq{i}", [128, 4, 256], BF16) for i in range(2)]

            def chain(qb, h, sl):
                d_ = slots[sl]
                po = d_["po"]; pTt = d_["pTt"]; o1 = d_["oT"]; ri = d_["rinv"]
                ob = obs[qb % 2]
                nk = 4 * qb + 4

                def emit_S(j):
                    jl = j - 4 * qb
                    c0 = 0 if jl < 0 else jl * 128
                    pS1 = d_["pS"][j % 2]
                    fw.op("pe", lambda e: e.matmul(pS1[:, c0:512], lhsT=kT[0:Dk, h, j * 128:(j + 1) * 128],
                                                   rhs=qT[0:Dk, h, qb * 512 + c0:(qb + 1) * 512],
                                                   start=True, stop=(jl < 0)), [kT.b, qT.b], [pS1.b])
                    if jl >= 0:
                        fw.op("pe", lambda e: e.matmul(pS1[:, c0:c0 + 128], lhsT=identb, rhs=cmb[:, negmask, :],
                                                       start=False, stop=True), [cmb.b], [pS1.b])
                    return pS1, c0

                pend = emit_S(0)
                yield
                for j in range(nk):
                    pS1, c0 = pend
                    if j + 1 < nk:
                        pend = emit_S(j + 1)
                    P = d_["P"][j % 2]
                    if bias_tb is None:
                        fw.op("act", lambda e: e.activation(out=P[:, c0:512], in_=pS1[:, c0:512], func=AF.Exp, scale=scale),
                              [pS1.b], [P.b])
                    else:
                        fw.op("act", lambda e: e.activation(out=P[:, c0:512], in_=pS1[:, c0:512], func=AF.Exp, scale=scale,
                                                            bias=bias_tb[:, j, h:h + 1]), [pS1.b, bias_tb.b], [P.b])
                    fw.op("pe", lambda e: e.matmul(po[0:65, c0:512], lhsT=Vc[:, j, h, :], rhs=P[:, c0:512],
                                                   start=(j == 0), stop=(j == nk - 1)), [Vc.b, P.b], [po.b])
                    yield
                fw.op("dve", lambda e: e.tensor_copy(out=o1[:], in_=po[0:65, :]), [po.b], [o1.b])
                yield
                for jq in range(4):
                    fw.op("pe", lambda e: e.transpose(out=pTt[:, jq * 66:jq * 66 + 65], in_=o1[:, jq * 128:(jq + 1) * 128],
                                                      identity=ident[0:65, 0:65]), [o1.b, cm.b], [pTt.b])
                yield
                fw.op("dve", lambda e: e.reciprocal(out=ri[:], in_=pTt[:, 0:264].rearrange("p (j c) -> p j c", c=66)[:, :, 64]),
                      [pTt.b], [ri.b])
                yield
                for jq in range(4):
                    fw.op("dve" if jq % 2 else "act",
                          (lambda e: e.tensor_scalar(out=ob[:, jq, h * 64:(h + 1) * 64], in0=pTt[:, jq * 66:jq * 66 + 64],
                                                     scalar1=ri[:, jq:jq + 1], scalar2=None, op0=ALU.mult)) if jq % 2 else
                          (lambda e: e.activation(out=ob[:, jq, h * 64:(h + 1) * 64], in_=pTt[:, jq * 66:jq * 66 + 64],
                                                  func=AF.Copy, scale=ri[:, jq:jq + 1])),
                          [pTt.b, ri.b], [ob.b])
                if h == 3:
                    yield
                    fw.dma("pool", obr_d[qb * 512:(qb + 1) * 512, col0:col0 + 256].rearrange("(j p) c -> p j c", p=128), ob[:],
                           reads=[ob.b])

            run_pipelined([(lambda sl, qb=qb, h=h: chain(qb, h, sl)) for qb in range(8) for h in range(4)], 2)

        def passB(l):
            with contextlib.ExitStack() as st:
                sb, ps = mk_alloc(st)
                colv = col_vectors(st, l)
                wq = sb("wq", [128, 2, 384], BF16)
                wkv = sb("wkv", [128, 512], BF16)
                stg = Ring([sb(f"stgB{i}", [128, 512]) for i in range(2)])
                for k in range(2):
                    load_cast(wq, wq[:, k, :], W["mla_w_q_up"][l, k * 128:(k + 1) * 128, :], stg, "dve",
                              colv[:, 16 + k:17 + k], [colv.b])
                load_cast(wkv, wkv[:], W["mla_w_kv_up"][l], stg, "dve", colv[:, 18:19], [colv.b])
                qT = sb("qTB", [96, 4, T], BF16)
                kT = sb("kTB", [96, 4, T], BF16)
                Vc = sb("VcB", [128, NT, 4, 65], BF16)
                fw.op("pool", lambda e: e.memset(Vc[:], 1.0), [], [Vc.b])
                with contextlib.ExitStack() as st2:
                    sb2, ps2 = mk_alloc(st2)
                    def mkslot(i):
                        return dict(pj=sb2(f"pjB{i}", [128, 416]), rp=sb2(f"rp{i}", [128, 256]), stat=sb2(f"statB{i}", [128, 4]),
                                    junk=sb2(f"junkB{i}", [128, 256]), cn=sb2(f"cn{i}", [128, 384], BF16),
                                    cnT=sb2(f"cnT{i}", [128, 3, 128], BF16), qf=sb2(f"qf{i}", [128, 4, 96], BF16),
                                    kf=sb2(f"kf{i}", [128, 4, 96], BF16), t1=sb2(f"t1{i}", [128, 4, 32]),
                                    t2=sb2(f"t2{i}", [128, 4, 32]), kp=sb2(f"kp{i}", [128, 32], BF16),
                                    t1k=sb2(f"t1k{i}", [128, 32]), t2k=sb2(f"t2k{i}", [128, 32]),
                                    pT=ps2(f"pTB{i}", [128, 1024], BF16), pq=ps2(f"pq{i}"), pkv=ps2(f"pkv{i}"))
                    slots = [mkslot(i) for i in range(2)]

                    def tile_gen(t, sl):
                            d_ = slots[sl]
                            pj = d_["pj"]; rp = d_["rp"]; stat = d_["stat"]; junk = d_["junk"]; cn = d_["cn"]; cnT = d_["cnT"]
                            qf = d_["qf"]; kf = d_["kf"]; t1 = d_["t1"]; t2 = d_["t2"]; kp = d_["kp"]; t1k = d_["t1k"]; t2k = d_["t2k"]
                            pT = d_["pT"]; pq = d_["pq"]; pkv = d_["pkv"]
                            fw.dma("sp", pj[:], proj_d[t * 128:(t + 1) * 128, O_CQ:O_CQ + 416], writes=[pj.b])
                            fw.dma("sp", rp[:], rope_d[t * 128:(t + 1) * 128, :], writes=[rp.b])
                            fw.op("act", lambda e: e.activation(out=junk[:, 0:256], in_=pj[:, 0:256], func=AF.Square,
                                                                accum_out=stat[:, 0:1]), [pj.b], [junk.b, stat.b])
                            fw.op("act", lambda e: e.activation(out=junk[:, 0:128], in_=pj[:, 256:384], func=AF.Square,
                                                                accum_out=stat[:, 1:2]), [pj.b], [junk.b, stat.b])
                            yield
                            rstd_from_ssq(stat[:, 0:1], stat[:, 2:3], 256, [stat.b])
                            rstd_from_ssq(stat[:, 1:2], stat[:, 3:4], 128, [stat.b])
                            yield
                            fw.op("dve", lambda e: e.tensor_scalar(out=cn[:, 0:256], in0=pj[:, 0:256], scalar1=stat[:, 2:3],
                                                                   scalar2=None, op0=ALU.mult), [pj.b, stat.b], [cn.b])
                            fw.op("dve", lambda e: e.tensor_scalar(out=cn[:, 256:384], in0=pj[:, 256:384], scalar1=stat[:, 3:4],
                                                                   scalar2=None, op0=ALU.mult), [pj.b, stat.b], [cn.b])
                            yield
                            for c in range(3):
                                fw.op("pe", lambda e: e.transpose(out=pT[:, c * 128:(c + 1) * 128], in_=cn[:, c * 128:(c + 1) * 128],
                                                                  identity=identb), [cn.b, cmb.b], [pT.b])
                            yield
                            fw.op("act", lambda e: e.copy(out=cnT[:].rearrange("p a b -> p (a b)"), in_=pT[:, 0:384]),
                                  [pT.b], [cnT.b])
                            yield
                            for k in range(2):
                                fw.op("pe", lambda e: e.matmul(pq[:, 0:384], lhsT=cnT[:, k, :], rhs=wq[:, k, :], start=(k == 0),
                                                               stop=(k == 1)), [cnT.b, wq.b], [pq.b])
                            fw.op("pe", lambda e: e.matmul(pkv[:, 0:512], lhsT=cnT[:, 2, :], rhs=wkv[:], start=True, stop=True),
                                  [cnT.b, wkv.b], [pkv.b])
                            yield
                            pq3 = pq[:, 0:384].rearrange("p (h d) -> p h d", h=4)
                            pkv3 = pkv[:, 0:512].rearrange("p (h d) -> p h d", h=4)
                            cos3 = rp[:, 0:128].rearrange("p (h d) -> p h d", h=4)
                            sin3 = rp[:, 128:256].rearrange("p (h d) -> p h d", h=4)
                            fw.op("act", lambda e: e.copy(out=qf[:, :, 0:64], in_=pq3[:, :, 0:64]), [pq.b], [qf.b])
                            fw.op("dve", lambda e: e.tensor_tensor(out=t1[:], in0=pq3[:, :, 64:96], in1=cos3, op=ALU.mult),
                                  [pq.b, rp.b], [t1.b])
                            fw.op("dve", lambda e: e.tensor_tensor(out=t2[:, :, 0:16], in0=pq3[:, :, 80:96], in1=sin3[:, :, 0:16],
                                                                   op=ALU.mult), [pq.b, rp.b], [t2.b])
                            fw.op("dve", lambda e: e.tensor_tensor(out=t2[:, :, 16:32], in0=pq3[:, :, 64:80], in1=sin3[:, :, 16:32],
                                                                   op=ALU.mult), [pq.b, rp.b], [t2.b])
                            yield
                            fw.op("pool", lambda e: e.tensor_tensor(out=qf[:, :, 64:96], in0=t1[:], in1=t2[:], op=ALU.add),
                                  [t1.b, t2.b], [qf.b])
                            fw.op("dve", lambda e: e.tensor_tensor(out=t1k[:], in0=pj[:, 384:416], in1=rp[:, 0:32], op=ALU.mult),
                                  [pj.b, rp.b], [t1k.b])
                            fw.op("dve", lambda e: e.tensor_tensor(out=t2k[:, 0:16], in0=pj[:, 400:416], in1=rp[:, 128:144],
                                                                   op=ALU.mult), [pj.b, rp.b], [t2k.b])
                            fw.op("dve", lambda e: e.tensor_tensor(out=t2k[:, 16:32], in0=pj[:, 384:400], in1=rp[:, 144:160],
                                                                   op=ALU.mult), [pj.b, rp.b], [t2k.b])
                            fw.op("pool", lambda e: e.tensor_tensor(out=kp[:], in0=t1k[:], in1=t2k[:], op=ALU.add),
                                  [t1k.b, t2k.b], [kp.b])
                            fw.op("act", lambda e: e.copy(out=kf[:, :, 0:64], in_=pkv3[:, :, 0:64]), [pkv.b], [kf.b])
                            for h in range(4):
                                fw.op("pool", lambda e: e.tensor_copy(out=kf[:, h, 64:96], in_=kp[:]), [kp.b], [kf.b])
                            fw.op("dve", lambda e: e.tensor_copy(out=Vc[:, t, :, 0:64], in_=pkv3[:, :, 64:128]), [pkv.b], [Vc.b])
                            yield
                            for h in range(4):
                                fw.op("pe", lambda e: e.transpose(out=pT[0:96, h * 128:(h + 1) * 128], in_=qf[:, h, :], identity=identb),
                                      [qf.b, cmb.b], [pT.b])
                                fw.op("pe", lambda e: e.transpose(out=pT[0:96, (4 + h) * 128:(5 + h) * 128], in_=kf[:, h, :],
                                                                  identity=identb), [kf.b, cmb.b], [pT.b])
                            yield
                            fw.op("act", lambda e: e.copy(out=qT[:, :, t * 128:(t + 1) * 128],
                                                          in_=pT[0:96, 0:512].rearrange("p (h n) -> p h n", h=4)), [pT.b], [qT.b])
                            fw.op("dve", lambda e: e.tensor_copy(out=kT[:, :, t * 128:(t + 1) * 128],
                                                                 in_=pT[0:96, 512:1024].rearrange("p (h n) -> p h n", h=4)),
                                  [pT.b], [kT.b])

                    run_pipelined([(lambda sl, t=t: tile_gen(t, sl)) for t in range(NT)], 2)
                    fw.barrier()
                with contextlib.ExitStack() as st3:
                    sb3, ps3 = mk_alloc(st3)
                    attention(sb3, ps3, qT, kT, Vc, 96, 96 ** -0.5, None, C_NCHUNK, 256)
                    fw.barrier()

        def passD(l):
            with contextlib.ExitStack() as st:
                sb, ps = mk_alloc(st)
                qT = sb("qTD", [67, 4, T], BF16)
                kT = sb("kTD", [67, 4, T], BF16)
                Vc = sb("VcD", [128, NT, 4, 65], BF16)
                fneg = sb("fneg", [128, NT, 4])
                fw.op("pool", lambda e: e.memset(Vc[:], 1.0), [], [Vc.b])
                with contextlib.ExitStack() as st2:
                    sb2, ps2 = mk_alloc(st2)
                    fb = bcast_load(sb2, "fbias", W["fox_f_bias"][l], 4)
                    acc = sb2("accD", [128, 4])
                    fw.op("pool", lambda e: e.memset(acc[:], 0.0), [], [acc.b])

                    def mkslot(i):
                        d_ = dict(pj=sb2(f"pjD{i}", [128, 772]), sp_=sb2(f"spD{i}", [128, 4]), f8=sb2(f"f8{i}", [128, 4]),
                                  pcb=sb2(f"pcb{i}", [128, 4], BF16), qa=sb2(f"qa{i}", [128, 4, 67], BF16),
                                  ka=sb2(f"ka{i}", [128, 4, 67], BF16), pT=ps2(f"pTD{i}", [128, 1024], BF16), pF=ps2(f"pF{i}"))
                        fw.op("pool", lambda e: e.memset(d_["ka"][:], 1.0), [], [d_["ka"].b])
                        return d_
                    slots = [mkslot(i) for i in range(2)]

                    def tile_gen(t, sl):
                            d_ = slots[sl]
                            pj = d_["pj"]; sp_ = d_["sp_"]; f8 = d_["f8"]; pcb = d_["pcb"]; qa = d_["qa"]; ka = d_["ka"]
                            pT = d_["pT"]; pF = d_["pF"]
                            fw.dma("sp", pj[:], proj_d[t * 128:(t + 1) * 128, O_FQ:O_FQ + 772], writes=[pj.b])
                            fw.op("dve", lambda e: e.tensor_tensor(out=sp_[:], in0=pj[:, 768:772], in1=fb[:], op=ALU.add),
                                  [pj.b, fb.b], [sp_.b])
                            fw.op("act", lambda e: e.activation(out=sp_[:], in_=sp_[:], func=AF.Exp, scale=-1.0), [sp_.b], [sp_.b])
                            fw.op("act", lambda e: e.activation(out=sp_[:], in_=sp_[:], func=AF.Ln, bias=1.0), [sp_.b], [sp_.b])
                            fw.op("pe", lambda e: e.matmul(pF[:, 0:4], lhsT=cm[:, C_CAUS, :], rhs=sp_[:], start=True, stop=False),
                                  [cm.b, sp_.b], [pF.b])
                            fw.op("pe", lambda e: e.matmul(pF[:, 0:4], lhsT=cm[:, C_ONES, :], rhs=acc[:], start=False, stop=True),
                                  [cm.b, acc.b], [pF.b])
                            fw.op("pool", lambda e: e.tensor_tensor(out=acc[:], in0=acc[:], in1=sp_[:], op=ALU.add),
                                  [acc.b, sp_.b], [acc.b])
                            yield
                            fw.op("act", lambda e: e.copy(out=fneg[:, t, :], in_=pF[:, 0:4]), [pF.b], [fneg.b])
                            fw.op("dve", lambda e: e.tensor_scalar(out=f8[:], in0=pF[:, 0:4], scalar1=-8.0, scalar2=None, op0=ALU.mult),
                                  [pF.b], [f8.b])
                            yield
                            fw.op("act", lambda e: e.copy(out=qa[:, :, 0:64], in_=pj[:, 0:256].rearrange("p (h d) -> p h d", h=4)),
                                  [pj.b], [qa.b])
                            fw.op("pool", lambda e: e.tensor_copy(out=ka[:, :, 0:64], in_=pj[:, 256:512].rearrange("p (h d) -> p h d", h=4)),
                                  [pj.b], [ka.b])
                            fw.op("pool", lambda e: e.tensor_copy(out=Vc[:, t, :, 0:64], in_=pj[:, 512:768].rearrange("p (h d) -> p h d", h=4)),
                                  [pj.b], [Vc.b])
                            yield
                            for i in range(3):
                                fw.op("dve", lambda e: e.tensor_copy(out=pcb[:], in_=f8[:]), [f8.b], [pcb.b])
                                fw.op("dve", lambda e: e.tensor_copy(out=qa[:, :, 64 + i], in_=pcb[:]), [pcb.b], [qa.b])
                                if i < 2:
                                    fw.op("dve", lambda e: e.tensor_tensor(out=f8[:], in0=f8[:], in1=pcb[:], op=ALU.subtract),
                                          [f8.b, pcb.b], [f8.b])
                            yield
                            for h in range(4):
                                fw.op("pe", lambda e: e.transpose(out=pT[0:67, h * 128:(h + 1) * 128], in_=qa[:, h, :], identity=identb),
                                      [qa.b, cmb.b], [pT.b])
                                fw.op("pe", lambda e: e.transpose(out=pT[0:67, (4 + h) * 128:(5 + h) * 128], in_=ka[:, h, :],
                                                                  identity=identb), [ka.b, cmb.b], [pT.b])
                            yield
                            fw.op("act", lambda e: e.copy(out=qT[:, :, t * 128:(t + 1) * 128],
                                                          in_=pT[0:67, 0:512].rearrange("p (h n) -> p h n", h=4)), [pT.b], [qT.b])
                            fw.op("dve", lambda e: e.tensor_copy(out=kT[:, :, t * 128:(t + 1) * 128],
                                                                 in_=pT[0:67, 512:1024].rearrange("p (h n) -> p h n", h=4)),
                                  [pT.b], [kT.b])

                    run_pipelined([(lambda sl, t=t: tile_gen(t, sl)) for t in range(NT)], 2)
                    fw.barrier()
                with contextlib.ExitStack() as st3:
                    sb3, ps3 = mk_alloc(st3)
                    attention(sb3, ps3, qT, kT, Vc, 67, 0.125, fneg, C_NCAUS, 768)
                    fw.barrier()

        def passC(l):
            with contextlib.ExitStack() as st:
                sb, ps = mk_alloc(st)
                gain_bc = bcast_load(sb, "gla_gain", W["gla_norm"][l], 64, 4)
                gbias = bcast_load(sb, "gla_gb", W["gla_gate_bias"][l], 128)
                wgu = sb("wgu", [16, 128])
                fw.dma("sp", wgu[:], W["gla_w_gate_up"][l], writes=[wgu.b])
                bdm = cv[:, 4:260]
                mask4 = sb("mask4", [128, 4, 128])
                for h in range(4):
                    fw.op("pool", lambda e: e.tensor_copy(out=mask4[:, h, :], in_=cm[:, C_BLKI, :]), [cm.b], [mask4.b])
                def mkslotC(i):
                    return dict(pj=sb(f"pjC{i}", [128, 784]), lrT=sb(f"lrT{i}", [16, 128]), la=sb(f"la{i}", [128, 128]),
                                bc=sb(f"bcum{i}", [128, 128]), eb=sb(f"eb{i}", [128, 128]), enb=sb(f"enb{i}", [128, 128]),
                                et=sb(f"et{i}", [128, 128]), qi=sb(f"qi{i}", [128, 128], BF16), ki=sb(f"ki{i}", [128, 128], BF16),
                                ktl=sb(f"ktl{i}", [128, 128], BF16), vbf=sb(f"vbf{i}", [128, 256], BF16),
                                kiT=sb(f"kiT{i}", [128, 128], BF16), qiT=sb(f"qiT{i}", [128, 128], BF16),
                                qm=sb(f"qm{i}", [128, 4, 128], BF16), at=sb(f"at{i}", [128, 4, 128], BF16),
                                cdc=sb(f"cdc{i}", [128, 2]), kvm=[sb(f"kvm{i}_{c}", [128, 256]) for c in range(2)],
                                o=sb(f"oC{i}", [128, 256]), oi=sb(f"oiC{i}", [128, 256]),
                                tmp=(sb(f"sqoC{i}", [128, 256]), sb(f"st4C{i}", [128, 8]), sb(f"sgC{i}", [128, 256]),
                                     sb(f"obC{i}", [128, 256], BF16)),
                                B1=ps(f"pCB1_{i}"), B2=ps(f"pCB2_{i}"), pT=ps(f"pTC{i}", [128, 1024], BF16), pI=ps(f"pCI{i}"))
                PC = [mkslotC(i) for i in range(2)]
                S = sb("SC", [128, 256]); Sb = sb("SCb", [128, 256], BF16)
                fw.op("pool", lambda e: e.memset(S[:], 0.0), [], [S.b])
                fw.op("pool", lambda e: e.memset(Sb[:], 0.0), [], [Sb.b])

                def tile_gen(t, sl):
                    d_ = PC[sl]
                    p1 = d_["B1"]; pA = d_["B1"]; p2 = d_["B2"]; pO = d_["B2"]; pKV = d_["B2"]; pT = d_["pT"]; pI = d_["pI"]
                    o = d_["o"]; tmp = d_["tmp"]
                    pj = d_["pj"]; lrT = d_["lrT"]; la = d_["la"]; bc = d_["bc"]; eb = d_["eb"]; enb = d_["enb"]; et = d_["et"]
                    qi = d_["qi"]; ki = d_["ki"]; ktl = d_["ktl"]; vbf = d_["vbf"]; kiT = d_["kiT"]; qiT = d_["qiT"]
                    qm = d_["qm"]; at = d_["at"]; cdc = d_["cdc"]; kvm = d_["kvm"]; oi = d_["oi"]
                    fw.dma("sp", pj[:], proj_d[t * 128:(t + 1) * 128, O_GQ:O_GQ + 784], writes=[pj.b])
                    yield
                    fw.op("pe", lambda e: e.transpose(out=p1[0:16, 0:128], in_=pj[:, 768:784], identity=ident), [pj.b, cm.b], [p1.b])
                    yield
                    fw.op("act", lambda e: e.copy(out=lrT[:], in_=p1[0:16, 0:128]), [p1.b], [lrT.b])
                    yield
                    fw.op("pe", lambda e: e.matmul(p1[:, 128:256], lhsT=lrT[:], rhs=wgu[:], start=True, stop=True),
                          [lrT.b, wgu.b], [p1.b])
                    yield
                    fw.op("dve", lambda e: e.tensor_tensor(out=la[:], in0=p1[:, 128:256], in1=gbias[:], op=ALU.add),
                          [p1.b, gbias.b], [la.b])
                    fw.op("act", lambda e: e.activation(out=la[:], in_=la[:], func=AF.Exp, scale=-1.0), [la.b], [la.b])
                    fw.op("act", lambda e: e.activation(out=la[:], in_=la[:], func=AF.Ln, bias=1.0), [la.b], [la.b])
                    fw.op("dve", lambda e: e.tensor_scalar(out=la[:], in0=la[:], scalar1=-1.0 / 16.0, scalar2=None, op0=ALU.mult),
                          [la.b], [la.b])
                    yield
                    fw.op("pe", lambda e: e.matmul(p2[:, 0:128], lhsT=cm[:, C_BLKI, :], rhs=la[:], start=True, stop=True),
                          [cm.b, la.b], [p2.b])
                    fw.op("pe", lambda e: e.matmul(p2[:, 128:256], lhsT=cm[:, C_SAME, :], rhs=la[:], start=True, stop=True),
                          [cm.b, la.b], [p2.b])
                    fw.op("pe", lambda e: e.matmul(p2[:, 256:258], lhsT=la[:], rhs=cm[:, C_SAME, 63:65], start=True, stop=True),
                          [cm.b, la.b], [p2.b])
                    yield
                    fw.op("act", lambda e: e.copy(out=bc[:], in_=p2[:, 0:128]), [p2.b], [bc.b])
                    fw.op("act", lambda e: e.activation(out=eb[:], in_=p2[:, 0:128], func=AF.Exp), [p2.b], [eb.b])
                    fw.op("act", lambda e: e.activation(out=enb[:], in_=p2[:, 0:128], func=AF.Exp, scale=-1.0), [p2.b], [enb.b])
                    fw.op("dve", lambda e: e.tensor_tensor(out=et[:], in0=p2[:, 128:256], in1=bc[:], op=ALU.subtract),
                          [p2.b, bc.b], [et.b])
                    fw.op("act", lambda e: e.activation(out=et[:], in_=et[:], func=AF.Exp), [et.b], [et.b])
                    fw.op("act", lambda e: e.activation(out=cdc[:], in_=p2[:, 256:258], func=AF.Exp), [p2.b], [cdc.b])
                    yield
                    fw.op("dve", lambda e: e.scalar_tensor_tensor(out=qi[:], in0=pj[:, 0:128], scalar=32 ** -0.5, in1=eb[:],
                                                                  op0=ALU.mult, op1=ALU.mult), [pj.b, eb.b], [qi.b])
                    fw.op("pool", lambda e: e.tensor_tensor(out=ki[:], in0=pj[:, 128:256], in1=enb[:], op=ALU.mult),
                          [pj.b, enb.b], [ki.b])
                    fw.op("pool", lambda e: e.tensor_tensor(out=ktl[:], in0=pj[:, 128:256], in1=et[:], op=ALU.mult),
                          [pj.b, et.b], [ktl.b])
                    fw.op("act", lambda e: e.copy(out=vbf[:], in_=pj[:, 256:512]), [pj.b], [vbf.b])
                    yield
                    fw.op("pe", lambda e: e.transpose(out=pT[:, 0:128], in_=qi[:], identity=identb), [qi.b, cmb.b], [pT.b])
                    fw.op("pe", lambda e: e.transpose(out=pT[:, 128:256], in_=ki[:], identity=identb), [ki.b, cmb.b], [pT.b])
                    yield
                    fw.op("act", lambda e: e.copy(out=qiT[:], in_=pT[:, 0:128]), [pT.b], [qiT.b])
                    fw.op("dve", lambda e: e.tensor_copy(out=kiT[:], in_=pT[:, 128:256]), [pT.b], [kiT.b])
                    for h in range(4):
                        fw.op("dve" if h % 2 else "act",
                              (lambda e: e.tensor_scalar(out=qm[:, h, :], in0=pT[:, 0:128], scalar1=cv[:, h:h + 1], scalar2=None,
                                                         op0=ALU.mult)) if h % 2 else
                              (lambda e: e.activation(out=qm[:, h, :], in_=pT[:, 0:128], func=AF.Copy, scale=cv[:, h:h + 1])),
                              [pT.b, cv.b], [qm.b])
                    yield
                    for h in range(4):
                        fw.op("pe", lambda e: e.matmul(pA[:, h * 128:(h + 1) * 128], lhsT=kiT[:], rhs=qm[:, h, :], start=True,
                                                       stop=True), [kiT.b, qm.b], [pA.b])
                    yield
                    fw.op("dve", lambda e: e.tensor_tensor(out=at[:].rearrange("p a b -> p (a b)"), in0=pA[:, 0:512],
                                                           in1=mask4[:].rearrange("p a b -> p (a b)"), op=ALU.mult),
                          [pA.b, mask4.b], [at.b])
                    for h in range(4):
                        fw.op("pe", lambda e: e.matmul(pO[:, h * 64:(h + 1) * 64], lhsT=at[:, h, :], rhs=vbf[:, h * 64:(h + 1) * 64],
                                                       start=True, stop=True), [at.b, vbf.b], [pO.b])
                    yield
                    fw.op("act", lambda e: e.copy(out=oi[:], in_=pO[:, 0:256]), [pO.b], [oi.b])
                    yield
                    for c in range(2):
                        r = slice(64 * c, 64 * c + 64)
                        fw.op("pe", lambda e: e.matmul(pKV[:, 0:256], lhsT=ktl[r, :], rhs=vbf[r, :], start=True, stop=True),
                              [ktl.b, vbf.b], [pKV.b])
                        fw.op("dve", lambda e: e.tensor_tensor(out=kvm[c][:], in0=pKV[:, 0:256], in1=bdm, op=ALU.mult),
                              [pKV.b, cv.b], [kvm[c].b])
                        yield

                    for c in range(2):
                        r = slice(64 * c, 64 * c + 64)
                        fw.op("pe", lambda e: e.matmul(pI[r, 0:256], lhsT=qiT[:, r], rhs=Sb[:], start=True, stop=True),
                              [qiT.b, Sb.b], [pI.b])
                        fw.op("dve", lambda e: e.scalar_tensor_tensor(out=S[:], in0=S[:], scalar=cdc[:, c:c + 1], in1=kvm[c][:],
                                                                       op0=ALU.mult, op1=ALU.add), [S.b, cdc.b, kvm[c].b], [S.b])
                        fw.op("act", lambda e: e.copy(out=Sb[:], in_=S[:]), [S.b], [Sb.b])
                        fw.op("dve", lambda e: e.tensor_tensor(out=o[r, :], in0=pI[r, 0:256], in1=oi[r, :], op=ALU.add),
                              [pI.b, oi.b], [o.b])
                    yield
                    for _ in head_norm_gate_gen(o, pj[:, 512:768], [pj.b], gain_bc, tmp, t, 512):
                        yield

                run_pipelined([(lambda sl, t=t: tile_gen(t, sl)) for t in range(NT)], 2, stagger=12)
                fw.barrier()

        def pass3(l, xsrc, xdst):
            with contextlib.ExitStack() as st:
                sb, ps = mk_alloc(st)
                colv = col_vectors(st, l)
                wg = [sb(f"wg{i}", [128, 8, D], BF16) for i in range(4)]
                wup = sb("wup", [128, 4, 2, D], BF16)
                wo = sb("wo", [128, 8, D], BF16)
                xin = [sb(f"x3{i}", [128, D]) for i in range(4)]
                hin = [sb(f"hT3{i}", [128, 8, 128], BF16) for i in range(4)]
                oin = [sb(f"ob3{i}", [128, D], BF16) for i in range(4)]
                stg = Ring(xin)
                n = 0
                for i in range(4):
                    for k in range(8):
                        load_cast(wg[i], wg[i][:, k, :], W["w_gate"][l, i, k * 128:(k + 1) * 128, :], stg,
                                  "act" if n % 2 else "dve", colv[:, k:k + 1], [colv.b])
                        n += 1
                for i in range(4):
                    for k in range(2):
                        fw.dma("pool", wup[:, i, k, :], W["w_branch_up"][l, i, k * 128:(k + 1) * 128, :], writes=[wup.b])
                for k in range(8):
                    fw.dma("pool", wo[:, k, :], W["w_out"][l, k * 128:(k + 1) * 128, :], writes=[wo.b])
                bg2 = sb("bg2", [1, 2, 4 * D], BF16)
                for i in range(4):
                    isl = slice(i * D, (i + 1) * D)
                    bgf = stg.next(); bgr = stg.next()
                    fw.dma("sp", bgf[0:1, :], W["b_gate"][l, i].partition_broadcast(1), writes=[bgf.b])
                    fw.op("dve", lambda e: e.tensor_copy(out=bg2[:, 0, isl], in_=bgf[0:1, :]), [bgf.b], [bg2.b])
                    fw.op("dve", lambda e: e.tensor_tensor(out=bgr[0:1, :], in0=bgf[0:1, :], in1=bg2[:, 0, isl], op=ALU.subtract),
                          [bgf.b, bg2.b], [bgr.b])
                    fw.op("dve", lambda e: e.tensor_copy(out=bg2[:, 1, isl], in_=bgr[0:1, :]), [bgr.b], [bg2.b])
                ones1 = sb("ones1", [1, 128], BF16)
                fw.op("pool", lambda e: e.memset(ones1[:], 1.0), [], [ones1.b])
                gpost = bcast_load(sb, "gpost", W["norm_mix_post"][l], D)
                pT = ps("pT3", [128, 1024], BF16)

                def mkslot(i):
                    return dict(oT=sb(f"oT3{i}", [128, 8, 128], BF16), gate=sb(f"gate{i}", [128, D], BF16), mrg=sb(f"mrg{i}", [128, D]),
                                tmpm=sb(f"tmpm{i}", [128, D]), mb=sb(f"mb{i}", [128, D], BF16), mT=sb(f"mT{i}", [128, 8, 128], BF16),
                                y=sb(f"y3{i}", [128, D]), stat=sb(f"stat3{i}", [128, 2]),
                                pg=ps(f"pg{i}"), pu=ps(f"pu{i}"), py=ps(f"py{i}"))
                slots = [mkslot(i) for i in range(2)]

                def tile_gen(t, sl):
                    d_ = slots[sl]
                    hTt = hin[t % 4]; obt = oin[t % 4]; xt = xin[t % 4]; oT = d_["oT"]; gate = d_["gate"]; mrg = d_["mrg"]
                    tmpm = d_["tmpm"]; mb = d_["mb"]; mT = d_["mT"]; y = d_["y"]; stat = d_["stat"]
                    pg = d_["pg"]; pu = d_["pu"]; py = d_["py"]
                    fw.dma("sp", hTt[:], hT_d[:, :, t * 128:(t + 1) * 128].rearrange("c p n -> p c n"), writes=[hTt.b])
                    fw.dma("sp", obt[:], obr_d[t * 128:(t + 1) * 128, :], writes=[obt.b])
                    fw.dma("sp", xt[:], xsrc[t * 128:(t + 1) * 128, :], writes=[xt.b])
                    yield
                    for c in range(8):
                        fw.op("pe", lambda e: e.transpose(out=pT[:, c * 128:(c + 1) * 128], in_=obt[:, c * 128:(c + 1) * 128],
                                                          identity=identb), [obt.b, cmb.b], [pT.b])
                    fw.op("act", lambda e: e.copy(out=oT[:].rearrange("p a b -> p (a b)"), in_=pT[:]), [pT.b], [oT.b])
                    yield
                    for i in range(4):
                        for hf in range(2):
                            cs = slice(hf * 512, (hf + 1) * 512)
                            for k in range(8):
                                fw.op("pe", lambda e: e.matmul(pg[:], lhsT=hTt[:, k, :], rhs=wg[i][:, k, cs], start=(k == 0),
                                                               stop=False), [hTt.b, wg[i].b], [pg.b])
                            fw.op("pe", lambda e: e.matmul(pg[:], lhsT=ones1[:],
                                                           rhs=bg2[:, 0, i * D + hf * 512:i * D + (hf + 1) * 512], start=False,
                                                           stop=False), [ones1.b, bg2.b], [pg.b])
                            fw.op("pe", lambda e: e.matmul(pg[:], lhsT=ones1[:],
                                                           rhs=bg2[:, 1, i * D + hf * 512:i * D + (hf + 1) * 512], start=False,
                                                           stop=True), [ones1.b, bg2.b], [pg.b])
                            for k in range(2):
                                fw.op("pe", lambda e: e.matmul(pu[:], lhsT=oT[:, 2 * i + k, :], rhs=wup[:, i, k, cs],
                                                               start=(k == 0), stop=(k == 1)), [oT.b, wup.b], [pu.b])
                            yield
                            fw.op("act", lambda e: e.activation(out=gate[:, cs], in_=pg[:], func=AF.Sigmoid),
                                  [pg.b], [gate.b])
                            if i == 0:
                                fw.op("dve", lambda e: e.tensor_tensor(out=mrg[:, cs], in0=pu[:], in1=gate[:, cs], op=ALU.mult),
                                      [pu.b, gate.b], [mrg.b])
                            else:
                                fw.op("dve", lambda e: e.tensor_tensor(out=tmpm[:, cs], in0=pu[:], in1=gate[:, cs], op=ALU.mult),
                                      [pu.b, gate.b], [tmpm.b])
                                if i < 3:
                                    fw.op("pool", lambda e: e.tensor_tensor(out=mrg[:, cs], in0=mrg[:, cs], in1=tmpm[:, cs],
                                                                            op=ALU.add), [mrg.b, tmpm.b], [mrg.b])
                                else:
                                    fw.op("pool", lambda e: e.tensor_tensor(out=mb[:, cs], in0=mrg[:, cs], in1=tmpm[:, cs],
                                                                            op=ALU.add), [mrg.b, tmpm.b], [mb.b])
                    yield
                    for c in range(8):
                        fw.op("pe", lambda e: e.transpose(out=pT[:, c * 128:(c + 1) * 128], in_=mb[:, c * 128:(c + 1) * 128],
                                                          identity=identb), [mb.b, cmb.b], [pT.b])
                    fw.op("act", lambda e: e.copy(out=mT[:].rearrange("p a b -> p (a b)"), in_=pT[:]), [pT.b], [mT.b])
                    yield
                    for hf in range(2):
                        cs = slice(hf * 512, (hf + 1) * 512)
                        for k in range(8):
                            fw.op("pe", lambda e: e.matmul(py[:], lhsT=mT[:, k, :], rhs=wo[:, k, cs], start=(k == 0),
                                                           stop=(k == 7)), [mT.b, wo.b], [py.b])
                        yield
                        fw.op("act" if hf else "dve",
                              (lambda e: e.copy(out=y[:, cs], in_=py[:])) if hf else
                              (lambda e: e.tensor_copy(out=y[:, cs], in_=py[:])), [py.b], [y.b])
                    yield
                    for _ in post_norm_residual(y, xt, gpost, stat, mb, xdst, t):
                        yield

                run_pipelined([(lambda sl, t=t: tile_gen(t, sl)) for t in range(NT)], 2)
                fw.barrier()

        def post_norm_residual(y, xt, gpost, stat, junk, xdst, t):
            fw.op("act", lambda e: e.activation(out=junk[:], in_=y[:], func=AF.Square, accum_out=stat[:, 0:1]),
                  [y.b], [junk.b, stat.b])
            yield
            rstd_from_ssq(stat[:, 0:1], stat[:, 1:2], D, [stat.b])
            yield
            fw.op("dve", lambda e: e.scalar_tensor_tensor(out=y[:], in0=y[:], scalar=stat[:, 1:2], in1=gpost[:], op0=ALU.mult,
                                                          op1=ALU.mult), [y.b, stat.b, gpost.b], [y.b])
            yield
            fw.op("pool", lambda e: e.tensor_tensor(out=xt[:], in0=y[:], in1=xt[:], op=ALU.add), [y.b, xt.b], [xt.b])
            fw.dma("pool", xdst[t * 128:(t + 1) * 128, :], xt[:], reads=[xt.b])

        def norm_gen(xt, xs, hTt, pT, stat):
            fw.op("act", lambda e: e.activation(out=xs[:], in_=xt[:], func=AF.Square, accum_out=stat[:, 0:1]),
                  [xt.b], [xs.b, stat.b])
            yield
            rstd_from_ssq(stat[:, 0:1], stat[:, 1:2], D, [stat.b])
            yield
            fw.op("dve", lambda e: e.tensor_scalar(out=xs[:], in0=xt[:], scalar1=stat[:, 1:2], scalar2=None,
                                                   op0=ALU.mult), [xt.b, stat.b], [xs.b])
            yield
            for c in range(8):
                fw.op("pe", lambda e: e.transpose(out=pT[:, c * 128:(c + 1) * 128], in_=xs[:, c * 128:(c + 1) * 128],
                                                  identity=identb), [xs.b, cmb.b], [pT.b])
            yield
            fw.op("act", lambda e: e.copy(out=hTt[:].rearrange("p a b -> p (a b)"), in_=pT[:]), [pT.b], [hTt.b])
            yield

        def pass4(l, xsrc, xdst):
            with contextlib.ExitStack() as st:
                sb, ps = mk_alloc(st)
                colv = col_vectors(st, l)
                w1 = [sb(f"w1_{q}", [128, 8, D], BF16) for q in range(4)]
                w2 = sb("w2", [128, 32, D], BF16)
                G = 2
                xr = [[sb(f"x4{s_}_{j}", [128, D]) for j in range(G)] for s_ in range(2)]
                stg = Ring([xr[0][0], xr[0][1], xr[1][0], xr[1][1]])
                n = 0
                for q4 in range(4):
                    for k in range(8):
                        load_cast(w1[q4], w1[q4][:, k, :],
                                  W["w_mlp_in"][l, k * 128:(k + 1) * 128, q4 * 1024:(q4 + 1) * 1024], stg,
                                  "act" if n % 2 else "dve", colv[:, 8 + k:9 + k], [colv.b])
                        n += 1
                for k in range(32):
                    fw.dma("pool", w2[:, k, :], W["w_mlp_out"][l, k * 128:(k + 1) * 128, :], writes=[w2.b])
                gpost = bcast_load(sb, "gpost4", W["norm_mlp_post"][l], D)
                xs = sb("xs4", [128, D], BF16)
                hT = [sb(f"hT4{s_}", [128, 8, G * 128], BF16) for s_ in range(2)]
                hTt = sb("hTt4", [128, 8, 128], BF16)
                uT = sb("uT", [128, 32, G * 128], BF16)
                rl = Ring([sb(f"rl{i}", [128, G * 128]) for i in range(2)])
                yr = Ring([sb(f"y4{i}", [128, D]) for i in range(2)]); junk = sb("junk4", [128, D], BF16)
                stat = [[sb(f"stat4{s_}_{j}", [128, 2]) for j in range(G)] for s_ in range(2)]
                stat2r = Ring([sb(f"stat4b{i}", [128, 2]) for i in range(2)])
                pT = ps("pT4", [128, 1024], BF16)
                pur = Ring([ps("pu40"), ps("pu41"), ps("pu42")])
                py = [ps("py40"), ps("py41")]
                NG = NT // G

                def s1_gen(g):
                    sl = g % 2
                    for j in range(G):
                        t = g * G + j
                        xt = xr[sl][j]
                        fw.dma("sp", xt[:], xsrc[t * 128:(t + 1) * 128, :], writes=[xt.b])
                        yield
                        for _ in norm_gen(xt, xs, hTt, pT, stat[sl][j]):
                            yield
                        fw.op("pool", lambda e: e.tensor_copy(out=hT[sl][:, :, j * 128:(j + 1) * 128], in_=hTt[:]),
                              [hTt.b], [hT[sl].b])
                        yield

                def s23_gen(g):
                    sl = g % 2
                    for f in range(32):
                        pu = pur.next()
                        for k in range(8):
                            fw.op("pe", lambda e: e.matmul(pu[:, 0:G * 128], lhsT=w1[f // 8][:, k, (f % 8) * 128:(f % 8 + 1) * 128],
                                                           rhs=hT[sl][:, k, :], start=(k == 0), stop=(k == 7)),
                                  [w1[f // 8].b, hT[sl].b], [pu.b])
                        r = rl.next()
                        fw.op("act", lambda e: e.activation(out=r[:], in_=pu[:, 0:G * 128], func=AF.Relu), [pu.b], [r.b])
                        fw.op("dve" if f % 2 else "pool", lambda e: e.tensor_tensor(out=uT[:, f, :], in0=r[:], in1=r[:], op=ALU.mult),
                              [r.b], [uT.b])
                        if f % 2:
                            yield
                    for j in range(G):
                        t = g * G + j
                        y = yr.next(); stat2 = stat2r.next()
                        for hf in range(2):
                            cs = slice(hf * 512, (hf + 1) * 512)
                            for f in range(32):
                                fw.op("pe", lambda e: e.matmul(py[hf][:], lhsT=uT[:, f, j * 128:(j + 1) * 128], rhs=w2[:, f, cs],
                                                               start=(f == 0), stop=(f == 31)), [uT.b, w2.b], [py[hf].b])
                                if f % 8 == 7:
                                    yield
                            fw.op("act" if hf else "dve",
                                  (lambda e: e.copy(out=y[:, cs], in_=py[hf][:])) if hf else
                                  (lambda e: e.tensor_copy(out=y[:, cs], in_=py[hf][:])), [py[hf].b], [y.b])
                        for _ in post_norm_residual(y, xr[sl][j], gpost, stat2, junk, xdst, t):
                            yield

                run_interleaved([s1_gen(0)])
                for g in range(NG):
                    gens = [s23_gen(g)]
                    if g + 1 < NG:
                        gens.append(s1_gen(g + 1))
                    run_interleaved(gens)
                fw.barrier()

        fw.barrier()
        for l in range(n_layers):
            xsrc = x_in if l == 0 else x2_d
            xdst = out_d if l == n_layers - 1 else x2_d
            if "1" in passes:
                pass1(l, xsrc)
            if "A" in passes:
                passA(l)
            if "B" in passes:
                passB(l)
            if "C" in passes:
                passC(l)
            if "D" in passes:
                passD(l)
            if "3" in passes:
                pass3(l, xsrc, x1_d)
            if "4" in passes:
                pass4(l, x1_d, xdst)
        fw.final_wait()
        print("instructions", fw.n_inst, "waits", fw.n_wait, flush=True)
    return nc


_CACHE = {}


def kernel(**inputs):
    cm, cv, rope = host_consts()
    if "nc" not in _CACHE:
        _CACHE["nc"] = build()
    nc = _CACHE["nc"]
    x = np.ascontiguousarray(inputs["x"], dtype=np.float32)
    in_maps = []
    for c in range(8):
        m = {k: np.ascontiguousarray(v, dtype=np.float32) for k, v in inputs.items() if k != "x"}
        m["x"] = np.ascontiguousarray(x[c])
        m["cmat"] = cm
        m["cvec"] = cv
        m["rope"] = rope
        in_maps.append(m)
    res = run_bass_kernel_spmd(nc, in_maps, core_ids=list(range(8)))
    return np.stack([np.asarray(r["out"], dtype=np.float32) for r in res.results], axis=0)
```
